# Optimizing a Trainium2 kernel written in Bass

```python
import jax, jax.numpy as jnp
from jax import lax
import numpy as np

D_MODEL = 1024
BATCH = 8
SEQ = 4096
DEPTH = 2

D_MIX = D_MODEL
D_FOURIER = D_MIX // 2
F_GROUPS = 4
F_GROUP_DIM = D_FOURIER // F_GROUPS
D_GLA_V = D_MIX - D_FOURIER
GLA_HEADS = 4
DV_HEAD = D_GLA_V // GLA_HEADS
DK_HEAD = DV_HEAD // 2
D_GLA_K = GLA_HEADS * DK_HEAD
GATE_RANK = 16
GATE_LOGIT_NORMALIZER = 16.0
CHUNK = 64
D_FF = -(-8 * D_MODEL // (3 * 256)) * 256
EPS = 1e-6
IN_SIZES = (D_FOURIER, D_GLA_K, D_GLA_K, D_GLA_V, D_GLA_V, GATE_RANK, GATE_RANK)
IN_COLS = sum(IN_SIZES)
IN_OFFSETS = tuple(int(o) for o in np.cumsum(IN_SIZES)[:-1])

kernel_name = "hybrid_fnet_gla_sandwich_encoder"


def rmsnorm(x, w):
    xf = x.astype(jnp.float32)
    y = xf * lax.rsqrt(jnp.mean(xf * xf, axis=-1, keepdims=True) + EPS)
    return (y * w.astype(jnp.float32)).astype(x.dtype)


def fourier_mix(u):
    b, s, _ = u.shape
    uf = u.astype(jnp.float32).reshape(b, s, F_GROUPS, F_GROUP_DIM)
    y = jnp.fft.fftn(uf, axes=(1, 3), norm="ortho").real
    return y.reshape(b, s, D_FOURIER).astype(u.dtype)


def gla_chunked(q, k, v, log_a):
    b, h, s, dk = q.shape
    dv = v.shape[-1]
    n = s // CHUNK
    q = q.reshape(b, h, n, CHUNK, dk)
    k = k.reshape(b, h, n, CHUNK, dk)
    v = v.reshape(b, h, n, CHUNK, dv)
    cum = jnp.cumsum(log_a.reshape(b, h, n, CHUNK, dk), axis=3)
    cum_last = cum[:, :, :, -1:, :]
    q_e = q * jnp.exp(cum)
    k_e = k * jnp.exp(-cum)
    k_end = k * jnp.exp(cum_last - cum)
    mask = jnp.tril(jnp.ones((CHUNK, CHUNK), dtype=bool))
    att = jnp.where(mask, jnp.einsum('bhnid,bhnjd->bhnij', q_e, k_e), 0.0)
    o_intra = jnp.einsum('bhnij,bhnjv->bhniv', att, v)
    chunk_state = jnp.einsum('bhnjd,bhnjv->bhndv', k_end, v)
    decay = jnp.exp(cum_last[:, :, :, 0, :])

    def step(state, inp):
        dec, cs = inp
        return state * dec[..., None] + cs, state

    init = jnp.zeros((b, h, dk, dv), q.dtype)
    _, prev = lax.scan(step, init, (jnp.moveaxis(decay, 2, 0), jnp.moveaxis(chunk_state, 2, 0)))
    prev = jnp.moveaxis(prev, 0, 2)
    o_inter = jnp.einsum('bhnid,bhndv->bhniv', q_e, prev)
    return (o_intra + o_inter).reshape(b, h, s, dv)


def gla_bidirectional(h_in, qp, kp, vp, gp, af, ab, w_af, b_af, w_ab, b_ab, w_onorm):
    b, s, _ = qp.shape
    dt = qp.dtype
    heads = lambda t, d: t.astype(jnp.float32).reshape(b, s, GLA_HEADS, d).transpose(0, 2, 1, 3)
    q = heads(qp, DK_HEAD) * (DK_HEAD ** -0.5)
    k = heads(kp, DK_HEAD)
    v = heads(vp, DV_HEAD)
    la_f = jax.nn.log_sigmoid((af @ w_af + b_af).astype(jnp.float32)) / GATE_LOGIT_NORMALIZER
    la_b = jax.nn.log_sigmoid((ab @ w_ab + b_ab).astype(jnp.float32)) / GATE_LOGIT_NORMALIZER
    la_f = heads(la_f, DK_HEAD)
    la_b = heads(la_b, DK_HEAD)
    o_fwd = gla_chunked(q, k, v, la_f)
    flip = lambda t: jnp.flip(t, axis=2)
    o_bwd = flip(gla_chunked(flip(q), flip(k), flip(v), flip(la_b)))
    o = (o_fwd + o_bwd).transpose(0, 2, 1, 3)
    o = rmsnorm(o, w_onorm)
    g = jax.nn.silu(gp.astype(jnp.float32)).reshape(b, s, GLA_HEADS, DV_HEAD)
    return (o * g).reshape(b, s, D_GLA_V).astype(dt)


def setup_inputs(seed: int = 0) -> dict:
    key = jax.random.key(seed)
    ks = jax.random.split(key, 16)
    nrm = lambda k, shape, scale: jax.random.normal(k, shape, jnp.float32) * scale
    gain = lambda k: 1.0 + 0.02 * jax.random.normal(k, (DEPTH, D_MODEL), jnp.float32)
    return {
        "x": jax.random.normal(ks[0], (BATCH, SEQ, D_MODEL), jnp.float32),
        "norm_mix_pre": gain(ks[1]),
        "w_in": nrm(ks[2], (DEPTH, D_MODEL, IN_COLS), D_MODEL ** -0.5),
        "w_alpha_fwd": nrm(ks[3], (DEPTH, GATE_RANK, D_GLA_K), GATE_RANK ** -0.5),
        "b_alpha_fwd": nrm(ks[4], (DEPTH, D_GLA_K), 0.1),
        "w_alpha_bwd": nrm(ks[5], (DEPTH, GATE_RANK, D_GLA_K), GATE_RANK ** -0.5),
        "b_alpha_bwd": nrm(ks[6], (DEPTH, D_GLA_K), 0.1),
        "gla_out_norm": 1.0 + 0.02 * jax.random.normal(ks[7], (DEPTH, DV_HEAD), jnp.float32),
        "w_out": nrm(ks[8], (DEPTH, D_MIX, D_MODEL), D_MIX ** -0.5),
        "norm_mix_post": gain(ks[9]),
        "norm_ffn_pre": gain(ks[10]),
        "w_ffn_gate": nrm(ks[11], (DEPTH, D_MODEL, D_FF), D_MODEL ** -0.5),
        "w_ffn_up": nrm(ks[12], (DEPTH, D_MODEL, D_FF), D_MODEL ** -0.5),
        "w_ffn_down": nrm(ks[13], (DEPTH, D_FF, D_MODEL), D_FF ** -0.5),
        "norm_ffn_post": gain(ks[14]),
    }


def reference(x, norm_mix_pre, w_in, w_alpha_fwd, b_alpha_fwd, w_alpha_bwd, b_alpha_bwd,
              gla_out_norm, w_out, norm_mix_post, norm_ffn_pre, w_ffn_gate, w_ffn_up,
              w_ffn_down, norm_ffn_post):
    for l in range(DEPTH):
        h = rmsnorm(x, norm_mix_pre[l])
        p = h @ w_in[l]
        fp, qp, kp, vp, gp, af, ab = jnp.split(p, IN_OFFSETS, axis=-1)
        y_f = fourier_mix(fp)
        y_g = gla_bidirectional(h, qp, kp, vp, gp, af, ab, w_alpha_fwd[l], b_alpha_fwd[l],
                                w_alpha_bwd[l], b_alpha_bwd[l], gla_out_norm[l])
        m = jnp.concatenate([y_f, y_g], axis=-1) @ w_out[l]
        x = x + rmsnorm(m, norm_mix_post[l])
        h2 = rmsnorm(x, norm_ffn_pre[l])
        f = (jax.nn.silu(h2 @ w_ffn_gate[l]) * (h2 @ w_ffn_up[l])) @ w_ffn_down[l]
        x = x + rmsnorm(f, norm_ffn_post[l])
    return x
```

```python
import numpy as np
import ml_dtypes
import concourse.bass as bass
import concourse.mybir as mybir
from concourse.bass_utils import run_bass_kernel_spmd

F32 = mybir.dt.float32
BF16 = mybir.dt.bfloat16
AF = mybir.ActivationFunctionType
ALU = mybir.AluOpType

S = 4096
D = 1024
NT = S // 128
DFF = 2816
NF = DFF // 128
INC = 2080
EPS = 1e-6
DEPTH = 2
STOP_AFTER = None
DEBUG = False

ENGS = ["pe", "act", "dve", "pool", "sp"]
NDSEM = 12


class Buf:
    __slots__ = ("ws", "rs")

    def __init__(self):
        self.ws = {}
        self.rs = {}


class T:
    __slots__ = ("ap", "b")

    def __init__(self, ap):
        self.ap = ap
        self.b = Buf()


class Ins:
    __slots__ = ("eng", "fn", "deps", "signal", "cnt", "dma", "dsem", "dval")

    def __init__(self, eng, fn, dma):
        self.eng = eng
        self.fn = fn
        self.deps = []
        self.signal = False
        self.cnt = 0
        self.dma = dma
        self.dsem = None
        self.dval = 0


def _key(ins):
    return (ins.eng, ins.dsem) if ins.dma else ins.eng


class Prog:
    def __init__(self, nc):
        self.nc = nc
        self.streams = {e: [] for e in ENGS}
        self.ndma = {e: 0 for e in ENGS}
        self.last = {}

    def op(self, eng, fn, reads=(), writes=(), dma=False):
        ins = Ins(eng, fn, dma)
        if dma:
            k = self.ndma[eng]
            self.ndma[eng] += 1
            ins.dsem = k % NDSEM
        deps = ins.deps
        for t in reads:
            for w in t.b.ws.values():
                deps.append((w, True))
        for t in writes:
            b = t.b
            for w in b.ws.values():
                deps.append((w, False))
            for r in b.rs.values():
                deps.append((r, False))
        for t in reads:
            t.b.rs[_key(ins)] = ins
        for t in writes:
            b = t.b
            if b.rs:
                b.ws = {}
                b.rs = {}
            b.ws[_key(ins)] = ins
        self.streams[eng].append(ins)
        self.last[_key(ins)] = ins
        return ins

    def barrier(self, skip_pool_dma=False):
        lasts = [x for x in self.last.values() if not (skip_pool_dma and x.dma and x.eng == "pool")]
        for e in ENGS:
            ins = Ins(e, None, False)
            ins.deps = [(x, True) for x in lasts]
            self.streams[e].append(ins)

    def emit(self):
        nc = self.nc
        from contextlib import ExitStack
        es = ExitStack()
        csem = {e: es.enter_context(nc.semaphore("c_" + e)) for e in ENGS}
        dsem = {e: [es.enter_context(nc.semaphore("d_%s_%d" % (e, i))) for i in range(min(NDSEM, self.ndma[e]))] for e in ENGS}

        def relevant(e, d, raw):
            if d.dma:
                return True
            if d.eng != e:
                return True
            if e == "pe":
                return False
            return raw

        for e in ENGS:
            for ins in self.streams[e]:
                for d, raw in ins.deps:
                    if not d.dma and relevant(e, d, raw):
                        d.signal = True
        for e in ENGS:
            c = 0
            dcount = [0] * NDSEM
            for ins in self.streams[e]:
                if ins.dma:
                    dcount[ins.dsem] += 16
                    ins.dval = dcount[ins.dsem]
                elif ins.signal:
                    c += 1
                    ins.cnt = c
        block = es.enter_context(nc.Block())

        def run(e):
            def body(eng):
                waited = {}
                for ins in self.streams[e]:
                    need = {}
                    for d, raw in ins.deps:
                        if not relevant(e, d, raw):
                            continue
                        if d.dma:
                            key = ("d", d.eng, d.dsem)
                            v = d.dval
                        else:
                            key = ("c", d.eng)
                            v = d.cnt
                        if v > need.get(key, 0):
                            need[key] = v
                    if ins.dma and ins.dval > 16:
                        key = ("d", e, ins.dsem)
                        need[key] = max(need.get(key, 0), ins.dval - 16)
                    for key, v in need.items():
                        if waited.get(key, 0) >= v:
                            continue
                        waited[key] = v
                        sem = csem[key[1]] if key[0] == "c" else dsem[key[1]][key[2]]
                        eng.wait_ge(sem, v)
                    if ins.fn is None:
                        continue
                    r = ins.fn(eng)
                    if ins.dma:
                        r.then_inc(dsem[e][ins.dsem], 16)
                    elif ins.signal:
                        r.then_inc(csem[e], 1)
                for i, s in enumerate(dsem[e]):
                    tot = 16 * len([1 for x in self.streams[e] if x.dma and x.dsem == i])
                    if tot:
                        eng.wait_ge(s, tot)
            return body

        block.tensor(run("pe"))
        block.scalar(run("act"))
        block.vector(run("dve"))
        block.gpsimd(run("pool"))
        block.sync(run("sp"))
        es.close()


class Alloc:
    def __init__(self, arena, lo, hi):
        self.arena = arena
        self.lo = lo
        self.hi = hi
        self.p = lo

    def f32(self, cols, parts=128):
        a = self.p
        self.p += cols
        assert self.p <= self.hi, (self.p, self.hi)
        return T(self.arena[0:parts, a:a + cols])

    def bf16(self, cols, parts=128):
        c = (cols + 1) // 2
        a = self.p
        self.p += c
        assert self.p <= self.hi, (self.p, self.hi)
        return T(self.arena[0:parts, a:a + c].bitcast(BF16))


ARENA_COLS = 52736
C_END = 1600
R1 = 1600
R2 = 17984
R3 = 26176


def build_program():
    nc = bass.Bass("TRN2", target_bir_lowering=False)
    dt_in = lambda name, shape, dt=F32: nc.dram_tensor(name, shape, dt, kind="ExternalInput").ap()
    x_in = dt_in("x", [S, D])
    norm_mix_pre = dt_in("norm_mix_pre", [DEPTH, D])
    w_in = dt_in("w_in", [DEPTH, D, INC])
    w_af = dt_in("w_alpha_fwd", [DEPTH, 16, 256])
    b_af = dt_in("b_alpha_fwd", [DEPTH, 256])
    w_ab = dt_in("w_alpha_bwd", [DEPTH, 16, 256])
    b_ab = dt_in("b_alpha_bwd", [DEPTH, 256])
    gla_on = dt_in("gla_out_norm", [DEPTH, 128])
    w_out = dt_in("w_out", [DEPTH, D, D])
    norm_mix_post = dt_in("norm_mix_post", [DEPTH, D])
    norm_ffn_pre = dt_in("norm_ffn_pre", [DEPTH, D])
    w_gate = dt_in("w_ffn_gate", [DEPTH, D, DFF])
    w_up = dt_in("w_ffn_up", [DEPTH, D, DFF])
    w_down = dt_in("w_ffn_down", [DEPTH, DFF, D])
    norm_ffn_post = dt_in("norm_ffn_post", [DEPTH, D])
    c_ident = dt_in("c_ident", [128, 128], BF16)
    c_ones = dt_in("c_ones", [128, 128], BF16)
    c_onesf = dt_in("c_onesf", [1, 128])
    c_dftc = dt_in("c_dftc", [128, 256], BF16)
    c_dfts = dt_in("c_dfts", [8, 2, 512, 512], BF16)
    c_tri = dt_in("c_tri", [2, 128, 128], BF16)
    c_mask = dt_in("c_mask", [2, 128, 512])
    y_out = nc.dram_tensor("y", [S, D], F32, kind="ExternalOutput").ap()
    scr = lambda name, shape, dt: nc.dram_tensor(name, shape, dt, kind="ExternalOutput" if DEBUG else "Internal").ap()
    xa_d = scr("xa_d", [S, D], F32)
    xb_d = scr("xb_d", [S, D], F32)
    qT_d = scr("qT_d", [4, 64, S], BF16)
    kT_d = scr("kT_d", [4, 64, S], BF16)
    k_d = scr("k_d", [S, 256], BF16)
    v_d = scr("v_d", [S, 512], BF16)
    sgT_d = scr("sgT_d", [4, 128, S], BF16)
    aT_d = scr("aT_d", [2, 17, S], BF16)
    dbgY = scr("dbgY", [4, 128, S], BF16) if DEBUG else None
    dbgG = scr("dbgG", [4, 128, S], BF16) if DEBUG else None
    D_xa, D_xb, D_qT, D_kT, D_k, D_v, D_sgT, D_aT, D_y = [T(None) for _ in range(9)]

    arena = nc.alloc_sbuf_tensor("arena", [128, ARENA_COLS], F32)
    psum = nc.alloc_psum_tensor("psum", [128, 4096], F32)
    PB = [T(psum[:, b * 512:(b + 1) * 512]) for b in range(8)]

    P = Prog(nc)
    ca = Alloc(arena, 0, C_END)
    identb = ca.bf16(128)
    onesb = ca.bf16(128)
    onesf = ca.f32(128, parts=1)
    nw = ca.f32(32)
    wpost = ca.f32(1024)
    wonorm = ca.f32(2)
    ss_all = ca.f32(256)
    ss_ctr = [0]

    def dma(q, out_ap, in_ap, reads=(), writes=(), slow=False):
        if slow:
            return P.op(q, lambda e: e.dma_start(out=out_ap, in_=in_ap, allow_slow_non_contiguous=True), reads=reads, writes=writes, dma=True)
        return P.op(q, lambda e: e.dma_start(out=out_ap, in_=in_ap), reads=reads, writes=writes, dma=True)

    def mm(out_t, out_ap, l_t, l_ap, r_t, r_ap, start, stop):
        return P.op("pe", lambda e: e.matmul(out_ap, lhsT=l_ap, rhs=r_ap, start=start, stop=stop), reads=[l_t, r_t], writes=[out_t])

    def act(out_t, out_ap, in_t, in_ap, func, extra_reads=(), **kw):
        return P.op("act", lambda e: e.activation(out=out_ap, in_=in_ap, func=func, **kw), reads=[in_t] + list(extra_reads), writes=[out_t])

    def tt(eng, out_t, out_ap, a_t, a_ap, b_t, b_ap, op):
        return P.op(eng, lambda e: e.tensor_tensor(out=out_ap, in0=a_ap, in1=b_ap, op=op), reads=[a_t, b_t], writes=[out_t])

    def ts(eng, out_t, out_ap, a_t, a_ap, s1, s2, op0, op1=None, extra_reads=()):
        if op1 is None:
            return P.op(eng, lambda e: e.tensor_scalar(out=out_ap, in0=a_ap, scalar1=s1, scalar2=None, op0=op0), reads=[a_t] + list(extra_reads), writes=[out_t])
        return P.op(eng, lambda e: e.tensor_scalar(out=out_ap, in0=a_ap, scalar1=s1, scalar2=s2, op0=op0, op1=op1), reads=[a_t] + list(extra_reads), writes=[out_t])

    def stt(eng, out_t, out_ap, a_t, a_ap, sc_t, sc_ap, b_t, b_ap, op0, op1):
        return P.op(eng, lambda e: e.scalar_tensor_tensor(out=out_ap, in0=a_ap, scalar=sc_ap, in1=b_ap, op0=op0, op1=op1), reads=[a_t, sc_t, b_t], writes=[out_t])

    def cp(eng, out_t, out_ap, in_t, in_ap):
        if eng == "act":
            return P.op("act", lambda e: e.copy(out=out_ap, in_=in_ap), reads=[in_t], writes=[out_t])
        return P.op(eng, lambda e: e.tensor_copy(out=out_ap, in_=in_ap), reads=[in_t], writes=[out_t])

    for a_ in range(2):
        dma("sp", aT_d[a_, 16:17, :], c_ones[0:32, :].rearrange("(o a) b -> o (a b)", o=1), writes=[D_aT])
    P.op("pool", lambda e: e.memset(ss_all.ap, 0.0), writes=[ss_all])
    dma("sp", identb.ap, c_ident, writes=[identb])
    dma("sp", onesb.ap, c_ones, writes=[onesb])
    dma("sp", onesf.ap, c_onesf, writes=[onesf])
    nwv = nw.ap.rearrange("p (l k c) -> p l k c", l=2, k=2)
    for l in range(DEPTH):
        dma("sp", nwv[:, l, 0, :], norm_mix_pre[l].rearrange("(c p) -> p c", p=128), writes=[nw], slow=True)
        dma("sp", nwv[:, l, 1, :], norm_ffn_pre[l].rearrange("(c p) -> p c", p=128), writes=[nw], slow=True)
        dma("sp", wonorm.ap[:, l:l + 1], gla_on[l].rearrange("(p o) -> p o", o=1), writes=[wonorm], slow=True)

    def load_weight(dst_t, dst_ap3, src, nchunk, ncols, stages, scale_ap=None, k0=0):
        engs = ["dve", "act"]
        for c in range(nchunk):
            st = stages[(k0 + c) % len(stages)]
            dma("sp", st.ap[:, 0:ncols], src[c * 128:(c + 1) * 128, :], writes=[st])
            eng = engs[(k0 + c) % 2]
            if scale_ap is not None:
                sc = scale_ap[:, c:c + 1]
                if eng == "act":
                    P.op("act", (lambda o, i, s_: lambda e: e.activation(out=o, in_=i, func=AF.Copy, scale=s_))(dst_ap3[:, c, :], st.ap[:, 0:ncols], sc), reads=[st, nw], writes=[dst_t])
                else:
                    ts(eng, dst_t, dst_ap3[:, c, :], st, st.ap[:, 0:ncols], sc, None, ALU.mult, extra_reads=[nw])
            else:
                cp(eng, dst_t, dst_ap3[:, c, :], st, st.ap[:, 0:ncols])

    def rmsnorm_stats(src_t, src_ap, junk, junk_ap):
        c = ss_ctr[0]
        ss_ctr[0] += 1
        col = ss_all.ap[:, c % 256:c % 256 + 1]
        sq = T(None)
        P.op("act", lambda e: e.activation(out=junk_ap, in_=src_ap, func=AF.Square, accum_out=col), reads=[src_t, ss_all], writes=[junk, sq])
        P.op("act", lambda e: e.activation(out=col, in_=col, func=AF.Sqrt, scale=1.0 / D, bias=EPS), reads=[sq], writes=[sq])
        P.op("dve", lambda e: e.reciprocal(out=col, in_=col), reads=[sq], writes=[sq])
        return sq, col

    def norm_hb(xt, hb):
        sq, col = rmsnorm_stats(xt, xt.ap, hb, hb.ap)
        ts("dve", hb, hb.ap, xt, xt.ap, col, None, ALU.mult, extra_reads=[sq])

    def transpose_to_hT(hb, tp, hT, hT_view, ceng, gain_ap=None):
        tpb = tp.ap.bitcast(BF16)
        for c in range(8):
            P.op("pe", (lambda o, i: lambda e: e.transpose(out=o, in_=i, identity=identb.ap))(tpb[:, c * 128:(c + 1) * 128], hb.ap[:, c * 128:(c + 1) * 128]), reads=[hb, identb], writes=[tp])
        if gain_ap is None:
            cp(ceng, hT, hT_view, tp, tpb.rearrange("p (c t) -> p c t", c=8))
        else:
            P.op("dve", (lambda o, a, b_: lambda e: e.tensor_tensor(out=o, in0=a, in1=b_, op=ALU.mult))(hT_view, tpb.rearrange("p (c t) -> p c t", c=8), gain_ap.unsqueeze(2).to_broadcast([128, 8, 128])), reads=[tp, nw], writes=[hT])

    def residual_out(m_t, m_ap, xt, t1, dst_ap, dst_T, q="pool"):
        sq, col = rmsnorm_stats(m_t, m_ap, t1, t1.ap)
        stt("dve", t1, t1.ap, m_t, m_ap, sq, col, wpost, wpost.ap, ALU.mult, ALU.mult)
        tt("dve", xt, xt.ap, xt, xt.ap, t1, t1.ap, ALU.add)
        dma(q, dst_ap, xt.ap, reads=[xt], writes=[dst_T])

    def stop_here(l, ph):
        return STOP_AFTER is not None and STOP_AFTER == (l, ph)

    def ring(al, n, cols, dt=F32, parts=128):
        return [al.f32(cols, parts) if dt == F32 else al.bf16(cols, parts) for _ in range(n)]

    done = False
    for l in range(DEPTH):
        if done:
            break
        x_src, X_src = (x_in, T(None)) if l == 0 else (xb_d, D_xb)
        x_fin, X_fin = (xb_d, D_xb) if l == 0 else (y_out, D_y)
        al = Alloc(arena, R2, ARENA_COLS)
        AB = T(arena[:, R1:R1 + 16384].bitcast(BF16))
        ABv = AB.ap.rearrange("p (t c) -> p t c", c=1024)
        winb_all = al.bf16(8 * INC)
        winv = winb_all.ap.rearrange("p (c n) -> p c n", c=8)
        WBLK = [(0, 512), (512, 1024), (1024, 1536), (1536, 2048), (2048, 2080)]
        WB = [T(winb_all.ap) for _ in WBLK]
        stages = [al.f32(544) for _ in range(4)]
        dftc = al.bf16(256)
        xts = [al.f32(1024) for _ in range(4)]
        hbs = [al.bf16(1024) for _ in range(4)]
        hTs = [al.bf16(8 * 512), al.bf16(8 * 512)]
        fpT = al.bf16(4 * 512)
        qst = al.bf16(2 * 512)
        kst = al.bf16(2 * 512)
        ktok = al.bf16(4 * 256)
        vtok = al.bf16(4 * 512)
        sgst = al.bf16(4 * 512)
        ast = al.bf16(512, parts=32)
        tmp = [al.f32(512) for _ in range(8)]
        dma("sp", dftc.ap, c_dftc, writes=[dftc])
        kkc = [0]

        def ld_wblk(bi):
            lo_, hi_ = WBLK[bi]
            for c in range(8):
                kk = kkc[0]
                st = stages[kk % 4]
                dma("sp", st.ap[:, 0:hi_ - lo_], w_in[l, c * 128:(c + 1) * 128, lo_:hi_], writes=[st])
                sc = nwv[:, l, 0, c:c + 1]
                if kk % 2:
                    P.op("act", (lambda o, i, s_: lambda e: e.activation(out=o, in_=i, func=AF.Copy, scale=s_))(winv[:, c, lo_:hi_], st.ap[:, 0:hi_ - lo_], sc), reads=[st, nw], writes=[WB[bi]])
                else:
                    ts("dve", WB[bi], winv[:, c, lo_:hi_], st, st.ap[:, 0:hi_ - lo_], sc, None, ALU.mult, extra_reads=[nw])
                kkc[0] += 1

        rot = [0]

        dftrot = [0]
        ABS = [T(AB.ap) for _ in range(32)]

        def nextbank(lo=2, n=3):
            b = PB[lo + rot[0] % n]
            rot[0] += 1
            return b

        def prep_norm(t):
            for j in range(4):
                tl = t + 8 * j
                xt = xts[j]
                dma("sp", xt.ap, x_src[tl * 128:(tl + 1) * 128, :], reads=[X_src], writes=[xt])
                norm_hb(xt, hbs[j])

        def prep_tr(t):
            hT = hTs[t % 2]
            hTv = hT.ap.rearrange("p (c t) -> p c t", c=8)
            for j in range(4):
                transpose_to_hT(hbs[j], PB[j % 2], hT, hTv[:, :, j * 128:(j + 1) * 128], "act")

        prep_norm(0)
        ld_wblk(0)
        prep_tr(0)
        ld_wblk(1)
        for t in range(8):
            hT = hTs[t % 2]
            hTv = hT.ap.rearrange("p (c t) -> p c t", c=8)
            if t + 1 < 8:
                prep_norm(t + 1)
            fpv = fpT.ap.rearrange("p (g t) -> p g t", g=4)
            for gg in range(4):
                pb = nextbank()
                for c in range(8):
                    mm(pb, pb.ap, WB[0], winv[:, c, gg * 128:(gg + 1) * 128], hT, hTv[:, c, :], c == 0, c == 7)
                cp("act" if gg % 2 else "dve", fpT, fpv[:, gg, :], pb, pb.ap)
            if t == 0:
                ld_wblk(2)
            for (off, stt_, dst_d, DT, scale) in ((512, qst, qT_d, D_qT, 0.125), (768, kst, kT_d, D_kT, 1.0)):
                sv = stt_.ap.rearrange("p (h t) -> p h t", h=2)
                for hp in range(2):
                    pb = nextbank()
                    for c in range(8):
                        mm(pb, pb.ap, WB[1], winv[:, c, off + hp * 128:off + (hp + 1) * 128], hT, hTv[:, c, :], c == 0, c == 7)
                    if hp % 2:
                        P.op("act", (lambda o, i, s_: lambda e: e.activation(out=o, in_=i, func=AF.Copy, scale=s_))(sv[:, hp, :], pb.ap, scale), reads=[pb], writes=[stt_])
                    else:
                        ts("dve", stt_, sv[:, hp, :], pb, pb.ap, scale, None, ALU.mult)
                for h in range(4):
                    dma("pool", dst_d[h].rearrange("d (j t p) -> d j t p", j=4, t=8)[:, :, t, :], sv[(h % 2) * 64:(h % 2) * 64 + 64, h // 2, :].rearrange("p (j q) -> p j q", j=4), reads=[stt_], writes=[DT])
            if t == 0:
                ld_wblk(3)
                ld_wblk(4)
            for j in range(4):
                tl = t + 8 * j
                for hh in range(2):
                    half = PB[5 + dftrot[0] % 3]
                    dftrot[0] += 1
                    for g2 in range(2):
                        gg = hh * 2 + g2
                        mm(half, half.ap[:, g2 * 256:g2 * 256 + 256], fpT, fpv[:, gg, j * 128:(j + 1) * 128], dftc, dftc.ap, True, True)
                    src = half.ap.rearrange("p (g a c) -> p a g c", g=2, a=2)
                    dst = ABv[:, tl, :].rearrange("p (a g c) -> p a g c", a=2, g=4)[:, :, hh * 2:hh * 2 + 2, :]
                    cp("act" if hh else "dve", ABS[tl], dst, half, src)
            A = [ABv[:, t + 8 * j, 0:512] for j in range(4)]
            Bq = [ABv[:, t + 8 * j, 512:1024] for j in range(4)]
            S_ = [ABS[t + 8 * j] for j in range(4)]
            eA, fA, gA, hA, eB, fB, gB, hB = tmp
            bfly = []

            def bf(o_t, o_ap, a_t, a_ap, b_t, b_ap, op):
                bfly.append(lambda: tt("dve", o_t, o_ap, a_t, a_ap, b_t, b_ap, op))

            def bf_stt(o_t, o_ap, p_t, p_ap, cst, b_t, b_ap):
                bfly.append(lambda: P.op("dve", lambda e: e.scalar_tensor_tensor(out=o_ap, in0=p_ap, scalar=cst, in1=b_ap, op0=ALU.mult, op1=ALU.add), reads=[p_t, b_t], writes=[o_t]))

            def bf_cp(o_t, o_ap, i_t, i_ap):
                bfly.append(lambda: cp("act", o_t, o_ap, i_t, i_ap))

            bf(eA, eA.ap, S_[0], A[0], S_[2], A[2], ALU.add)
            bf(fA, fA.ap, S_[0], A[0], S_[2], A[2], ALU.subtract)
            bf(gA, gA.ap, S_[1], A[1], S_[3], A[3], ALU.add)
            bf(hA, hA.ap, S_[1], A[1], S_[3], A[3], ALU.subtract)
            bf(eB, eB.ap, S_[0], Bq[0], S_[2], Bq[2], ALU.add)
            bf(fB, fB.ap, S_[0], Bq[0], S_[2], Bq[2], ALU.subtract)
            bf(gB, gB.ap, S_[1], Bq[1], S_[3], Bq[3], ALU.add)
            bf(hB, hB.ap, S_[1], Bq[1], S_[3], Bq[3], ALU.subtract)
            bf(S_[0], A[0], eA, eA.ap, gA, gA.ap, ALU.add)
            bf(S_[0], Bq[0], eB, eB.ap, gB, gB.ap, ALU.add)
            bf(S_[1], A[1], fA, fA.ap, hB, hB.ap, ALU.subtract)
            bf(S_[1], Bq[1], fB, fB.ap, hA, hA.ap, ALU.add)
            bf(S_[2], A[2], eA, eA.ap, gA, gA.ap, ALU.subtract)
            bf(S_[2], Bq[2], gB, gB.ap, eB, eB.ap, ALU.subtract)
            bf(S_[3], A[3], fA, fA.ap, hB, hB.ap, ALU.add)
            bf(S_[3], Bq[3], hA, hA.ap, fB, fB.ap, ALU.subtract)

            if t >= 4:
                t2 = t - 4
                c8 = float(np.sqrt(0.5))
                for r_ in range(4):
                    lo_s, hi_s = t2 + 8 * r_, t2 + 4 + 8 * r_
                    Lt, Ht = ABS[lo_s], ABS[hi_s]
                    aL, bL = ABv[:, lo_s, 0:512], ABv[:, lo_s, 512:1024]
                    aH, bH = ABv[:, hi_s, 0:512], ABv[:, hi_s, 512:1024]
                    u1, u2 = tmp[(2 * r_) % 8], tmp[(2 * r_ + 1) % 8]
                    if r_ == 0:
                        bf(u1, u1.ap, Lt, aL, Ht, aH, ALU.subtract)
                        bf(u2, u2.ap, Lt, bL, Ht, bH, ALU.subtract)
                        bf(Lt, aL, Lt, aL, Ht, aH, ALU.add)
                        bf(Lt, bL, Lt, bL, Ht, bH, ALU.add)
                        bf_cp(Ht, aH, u1, u1.ap)
                        bf_cp(Ht, bH, u2, u2.ap)
                    elif r_ == 2:
                        bf(u1, u1.ap, Lt, aL, Ht, bH, ALU.subtract)
                        bf(u2, u2.ap, Lt, bL, Ht, aH, ALU.add)
                        bf(Lt, aL, Lt, aL, Ht, bH, ALU.add)
                        bf(Lt, bL, Lt, bL, Ht, aH, ALU.subtract)
                        bf_cp(Ht, aH, u1, u1.ap)
                        bf_cp(Ht, bH, u2, u2.ap)
                    else:
                        sg_ = 1.0 if r_ == 1 else -1.0
                        bf(u1, u1.ap, Ht, aH, Ht, bH, ALU.subtract)
                        bf(u2, u2.ap, Ht, aH, Ht, bH, ALU.add)
                        bf_stt(Ht, aH, u1, u1.ap, -sg_ * c8, Lt, aL)
                        bf_stt(Ht, bH, u2, u2.ap, -sg_ * c8, Lt, bL)
                        bf_stt(Lt, aL, u1, u1.ap, sg_ * c8, Lt, aL)
                        bf_stt(Lt, bL, u2, u2.ap, sg_ * c8, Lt, bL)

            def emit_bf(n_):
                for _ in range(n_):
                    if bfly:
                        bfly.pop(0)()

            if t + 1 < 8:
                prep_tr(t + 1)
            kv = ktok.ap.rearrange("p (j n) -> p j n", j=4)
            vv = vtok.ap.rearrange("p (j n) -> p j n", j=4)
            ksv = kst.ap.rearrange("p (h t) -> p h t", h=2)
            pbk = nextbank()
            pbk16 = pbk.ap.bitcast(BF16).rearrange("p (j h n) -> p j h n", j=4, h=2)
            for j in range(4):
                for hp in range(2):
                    P.op("pe", (lambda o, i: lambda e: e.transpose(out=o, in_=i, identity=identb.ap))(pbk16[:, j, hp, :], ksv[:, hp, j * 128:(j + 1) * 128]), reads=[kst, identb], writes=[pbk])
            cp("act", ktok, ktok.ap, pbk, pbk.ap.bitcast(BF16))
            for j in range(4):
                pb = nextbank()
                for c in range(8):
                    mm(pb, pb.ap, hT, hTv[:, c, j * 128:(j + 1) * 128], WB[2], winv[:, c, 1024:1536], c == 0, c == 7)
                cp("act", vtok, vv[:, j, :], pb, pb.ap)
                emit_bf(6)
            dma("pool", k_d.rearrange("(j t p) n -> p j t n", j=4, t=8)[:, :, t, :], kv, reads=[ktok], writes=[D_k])
            dma("pool", v_d.rearrange("(j t p) n -> p j t n", j=4, t=8)[:, :, t, :], vv, reads=[vtok], writes=[D_v])
            sgv = sgst.ap.rearrange("p (h t) -> p h t", h=4)
            for h in range(4):
                pb = nextbank()
                for c in range(8):
                    mm(pb, pb.ap, WB[3], winv[:, c, 1536 + h * 128:1536 + (h + 1) * 128], hT, hTv[:, c, :], c == 0, c == 7)
                act(sgst, sgv[:, h, :], pb, pb.ap, AF.Silu)
                emit_bf(3)
            for h in range(4):
                dma("pool", sgT_d[h].rearrange("d (j t p) -> d j t p", j=4, t=8)[:, :, t, :], sgv[:, h, :].rearrange("p (j q) -> p j q", j=4), reads=[sgst], writes=[D_sgT])
            pb = nextbank()
            for c in range(8):
                mm(pb, pb.ap[0:32, :], WB[4], winv[:, c, 2048:2080], hT, hTv[:, c, :], c == 0, c == 7)
            cp("dve", ast, ast.ap, pb, pb.ap[0:32, :])
            emit_bf(64)
            for a in range(2):
                dma("pool", aT_d[a, 0:16, :].rearrange("r (j t p) -> r j t p", j=4, t=8)[:, :, t, :], ast.ap[a * 16:(a + 1) * 16, :].rearrange("p (j q) -> p j q", j=4), reads=[ast], writes=[D_aT])
        MPRE = 48640
        assert al.p <= MPRE
        mat_pre = [T(arena[:, MPRE + i * 1024:MPRE + (i + 1) * 1024].bitcast(BF16)) for i in range(2)]
        for ab in range(2):
            dma("sp", mat_pre[ab].ap.rearrange("p (t k) -> p t k", t=4), c_dfts[0, ab].rearrange("(t p) k -> p t k", p=128), writes=[mat_pre[ab]])
        P.barrier()
        if stop_here(l, "A"):
            done = True
            break

        al = Alloc(arena, R3, MPRE - 2048)
        tri = al.bf16(256)
        mask = al.f32(1024)
        walpha = al.bf16(512, parts=17)
        dma("sp", tri.ap.rearrange("p (a t) -> p a t", a=2), c_tri.rearrange("a p t -> p a t"), writes=[tri])
        dma("sp", mask.ap.rearrange("p (a t) -> p a t", a=2), c_mask.rearrange("a p t -> p a t"), writes=[mask])
        dma("pool", walpha.ap[0:16, 0:256], w_af[l], writes=[walpha])
        dma("pool", walpha.ap[0:16, 256:512], w_ab[l], writes=[walpha])
        dma("pool", walpha.ap[16:17, 0:256], b_af[l:l + 1, :], writes=[walpha])
        dma("pool", walpha.ap[16:17, 256:512], b_ab[l:l + 1, :], writes=[walpha])
        c_alloc_next = al.p
        YT = T(arena[:, R2:R2 + 8192].bitcast(BF16))
        YTv = YT.ap.rearrange("p (g k r) -> p g k r", g=4, r=8)
        mats = mat_pre + [T(arena[:, MPRE - 2048 + i * 1024:MPRE - 2048 + (i + 1) * 1024].bitcast(BF16)) for i in range(2)]
        rot[0] = 0
        for rho in range(8):
            r_, q_ = rho % 4, rho // 4
            mre, mim = mats[(rho % 2) * 2], mats[(rho % 2) * 2 + 1]
            if rho > 0:
                for ab, mt in ((0, mre), (1, mim)):
                    dma("sp", mt.ap.rearrange("p (t k) -> p t k", t=4), c_dfts[rho, ab].rearrange("(t p) k -> p t k", p=128), writes=[mt])
            mrev = mre.ap.rearrange("p (t k) -> p t k", t=4)
            mimv = mim.ap.rearrange("p (t k) -> p t k", t=4)
            for gg in range(4):
                pb = nextbank(0, 8)
                for t2 in range(4):
                    sl_ = t2 + 4 * q_ + 8 * r_
                    mm(pb, pb.ap, ABS[sl_], ABv[:, sl_, gg * 128:(gg + 1) * 128], mre, mrev[:, t2, :], t2 == 0, False)
                    mm(pb, pb.ap, ABS[sl_], ABv[:, sl_, 512 + gg * 128:512 + (gg + 1) * 128], mim, mimv[:, t2, :], False, t2 == 3)
                cp("act" if gg % 2 else "dve", YT, YTv[:, gg, :, rho], pb, pb.ap)
        if stop_here(l, "B"):
            P.barrier()
            done = True
            break

        al = Alloc(arena, c_alloc_next, MPRE - 2048)
        ygT = T(arena[:, R1:R1 + 8192].bitcast(BF16))
        OT = T(arena[:, R1 + 8192:R1 + 16384].bitcast(BF16))
        ygv = ygT.ap.rearrange("p (h s) -> p h s", h=4)
        OTv = OT.ap.rearrange("p (h s) -> p h s", h=4)
        R_aT = ring(al, 4, 128, BF16, 17)
        R_kt = ring(al, 4, 256, BF16)
        R_qT = ring(al, 4, 256, BF16)
        R_kT = ring(al, 4, 256, BF16)
        R_v = ring(al, 7, 512, BF16)
        R_sg = ring(al, 3, 512, BF16)
        R_ex = ring(al, 2, 256)
        R_lap = ring(al, 3, 256, BF16)
        R_Ek = ring(al, 3, 256)
        R_ke = ring(al, 3, 256, BF16)
        R_EqT = ring(al, 5, 256)
        R_EkT = ring(al, 3, 256)
        R_qeT = ring(al, 6, 256, BF16)
        R_keT = ring(al, 3, 256, BF16)
        R_att = ring(al, 4, 512, BF16)
        R_prev = ring(al, 4, 256, BF16)
        R_tmps = ring(al, 2, 256)
        R_osum = ring(al, 5, 512)
        R_sq = ring(al, 3, 512, BF16)
        R_rstd = ring(al, 3, 512)
        R_t1 = ring(al, 2, 512)
        state = al.f32(256)
        NG = 2 * NT

        def r2(ap):
            return ap.rearrange("p (r t) -> p r t", r=2)

        def gi_info(gi):
            dr = gi // NT
            ci = gi % NT
            n = ci if dr == 0 else NT - 1 - ci
            return dr, n, slice(n * 128, (n + 1) * 128)

        def pick(rg, gi):
            return rg[gi % len(rg)]

        def h4(ap):
            return ap.rearrange("p (h t) -> p h t", h=4)

        def pl_of(gi):
            return PB[0], PB[0].ap[:, 0:256]

        def pc_of(gi):
            return PB[1], PB[1].ap[:, 0:256]

        PCT = PB[1].ap[:, 256:512]

        def L_aT(gi):
            dr, n, tok = gi_info(gi)
            b_ = pick(R_aT, gi)
            dma("sp", b_.ap, aT_d[dr, :, tok], reads=[D_aT], writes=[b_])

        def L_qk(gi):
            dr, n, tok = gi_info(gi)
            b_ = pick(R_qT, gi)
            dma("sp", r2(b_.ap), qT_d.rearrange("(r e) d t -> (e d) r t", e=2)[:, :, tok], reads=[D_qT], writes=[b_])
            b_ = pick(R_kT, gi)
            dma("sp", r2(b_.ap), kT_d.rearrange("(r e) d t -> (e d) r t", e=2)[:, :, tok], reads=[D_kT], writes=[b_])
            b_ = pick(R_kt, gi)
            dma("sp", b_.ap, k_d[tok, :], reads=[D_k], writes=[b_])

        def L_v(gi):
            dr, n, tok = gi_info(gi)
            b_ = pick(R_v, gi)
            dma("sp", b_.ap, v_d[tok, :], reads=[D_v], writes=[b_])

        def L_sg(gi):
            dr, n, tok = gi_info(gi)
            if dr == 1:
                b_ = pick(R_sg, gi)
                dma("sp", h4(b_.ap), sgT_d[:, :, tok].rearrange("h d t -> d h t"), reads=[D_sgT], writes=[b_])

        def s1(gi):
            dr, n, tok = gi_info(gi)
            aT = pick(R_aT, gi)
            plT, pl = pl_of(gi)
            mm(plT, pl, aT, aT.ap, walpha, walpha.ap[:, dr * 256:(dr + 1) * 256], True, True)

        def s2(gi):
            ex, lap = pick(R_ex, gi), pick(R_lap, gi)
            plT, pl = pl_of(gi)
            act(ex, ex.ap, plT, pl, AF.Exp, scale=-1.0)
            act(lap, lap.ap, ex, ex.ap, AF.Ln, bias=1.0, scale=1.0)

        def s3(gi):
            dr, n, tok = gi_info(gi)
            lap = pick(R_lap, gi)
            triap = tri.ap[:, dr * 128:(dr + 1) * 128]
            pcT, pc = pc_of(gi)
            mm(pcT, pc, tri, triap, lap, lap.ap, True, True)
            for pr in range(2):
                mm(PB[1], PCT[:, pr * 128:(pr + 1) * 128], lap, lap.ap[:, pr * 128:(pr + 1) * 128], tri, triap, True, True)

        def s4(gi):
            Ek, EqT, EkT = pick(R_Ek, gi), pick(R_EqT, gi), pick(R_EkT, gi)
            pcT, pc = pc_of(gi)
            act(EqT, EqT.ap, PB[1], PCT, AF.Exp, scale=-1.0 / 16)
            act(EkT, EkT.ap, PB[1], PCT, AF.Exp, scale=1.0 / 16)
            act(Ek, Ek.ap, pcT, pc, AF.Exp, scale=1.0 / 16)

        def s5(gi):
            kt, Ek, ke = pick(R_kt, gi), pick(R_Ek, gi), pick(R_ke, gi)
            qT, kT, EqT, EkT, qeT, keT = pick(R_qT, gi), pick(R_kT, gi), pick(R_EqT, gi), pick(R_EkT, gi), pick(R_qeT, gi), pick(R_keT, gi)
            tt("dve", qeT, qeT.ap, qT, qT.ap, EqT, EqT.ap, ALU.mult)
            tt("pool", keT, keT.ap, kT, kT.ap, EkT, EkT.ap, ALU.mult)
            tt("pool", ke, ke.ap, kt, kt.ap, Ek, Ek.ap, ALU.mult)

        def s6(gi):
            keT, qeT, ke, v = pick(R_keT, gi), pick(R_qeT, gi), pick(R_ke, gi), pick(R_v, gi)
            for h in range(4):
                pr, e_ = h // 2, h % 2
                ps_, cs_ = slice(e_ * 64, (e_ + 1) * 64), slice(pr * 128, (pr + 1) * 128)
                pa = PB[2 + e_]
                mm(pa, pa.ap[:, cs_], keT, keT.ap[ps_, cs_], qeT, qeT.ap[ps_, cs_], True, True)
            pk = PB[4]
            for h in range(4):
                pr = h // 2
                mm(pk, pk.ap[:, h * 128:(h + 1) * 128], ke, ke.ap[:, pr * 128:(pr + 1) * 128], v, v.ap[:, h * 128:(h + 1) * 128], True, True)

        def s7(gi):
            dr, n, tok = gi_info(gi)
            if gi % NT == 0:
                P.op("pool", lambda e: e.memset(state.ap, 0.0), writes=[state])
                pv0 = pick(R_prev, gi)
                P.op("pool", (lambda o: lambda e: e.memset(o, 0.0))(pv0.ap), writes=[pv0])
            dcol = 127 if dr == 0 else 0
            att, EqT, tmps = pick(R_att, gi), pick(R_EqT, gi), pick(R_tmps, gi)
            pk = PB[4]
            for e_ in range(2):
                ps_ = slice(e_ * 64, (e_ + 1) * 64)
                P.op("dve", (lambda o, a, b_: lambda e: e.tensor_tensor(out=o, in0=a, in1=b_, op=ALU.add))(r2(tmps.ap)[ps_], r2(state.ap)[ps_], pk.ap[ps_, :].rearrange("p (r f t) -> p r f t", r=2, f=2)[:, :, e_, :]), reads=[state, pk], writes=[tmps])
            for e_ in range(2):
                pa = PB[2 + e_]
                P.op("dve", (lambda o, a, b_: lambda e: e.tensor_tensor(out=o, in0=a, in1=b_, op=ALU.mult))(att.ap.rearrange("p (r f t) -> p r f t", r=2, f=2)[:, :, e_, :], r2(pa.ap[:, 0:256]), r2(mask.ap[:, dr * 512:dr * 512 + 256])), reads=[pa, mask], writes=[att])
            P.op("dve", (lambda o, a, b_: lambda e: e.tensor_tensor(out=o, in0=a, in1=b_, op=ALU.mult))(r2(state.ap), r2(tmps.ap), r2(EqT.ap)[:, :, dcol:dcol + 1].to_broadcast([128, 2, 128])), reads=[tmps, EqT], writes=[state])

        def s8(gi):
            pvn = pick(R_prev, gi + 1)
            cp("act", pvn, pvn.ap, state, state.ap)

        def s9(gi):
            v, att, pv, qeT = pick(R_v, gi), pick(R_att, gi), pick(R_prev, gi), pick(R_qeT, gi)
            po = PB[5 + gi % 2]
            for h in range(4):
                hs = slice(h * 128, (h + 1) * 128)
                pr, e_ = h // 2, h % 2
                ps_, cs_ = slice(e_ * 64, (e_ + 1) * 64), slice(pr * 128, (pr + 1) * 128)
                mm(po, po.ap[:, hs], v, v.ap[:, hs], att, att.ap[:, hs], True, False)
                mm(po, po.ap[:, hs], pv, pv.ap[ps_, cs_], qeT, qeT.ap[ps_, cs_], False, True)

        def s10(gi):
            dr, n, tok = gi_info(gi)
            po = PB[5 + gi % 2]
            if dr == 0:
                cp("act", OT, OTv[:, :, tok], po, h4(po.ap))
                return
            osum, sq = pick(R_osum, gi), pick(R_sq, gi)
            P.op("dve", (lambda o, a, b_: lambda e: e.tensor_tensor(out=o, in0=a, in1=b_, op=ALU.add))(h4(osum.ap), h4(po.ap), OTv[:, :, tok]), reads=[po, OT], writes=[osum])
            act(sq, sq.ap, osum, osum.ap, AF.Square)

        def s11(gi):
            dr, n, tok = gi_info(gi)
            if dr == 0:
                return
            sq = pick(R_sq, gi)
            pn = PB[7]
            mm(pn, pn.ap, onesb, onesb.ap, sq, sq.ap, True, True)

        def s12(gi):
            dr, n, tok = gi_info(gi)
            if dr == 0:
                return
            rstd = pick(R_rstd, gi)
            pn = PB[7]
            act(rstd, rstd.ap, pn, pn.ap, AF.Ln, scale=1.0 / 128, bias=EPS)
            act(rstd, rstd.ap, rstd, rstd.ap, AF.Exp, scale=-0.5)

        def s13(gi):
            dr, n, tok = gi_info(gi)
            if dr == 0:
                return
            osum, rstd, t1, sg = pick(R_osum, gi), pick(R_rstd, gi), pick(R_t1, gi), pick(R_sg, gi)
            stt("dve", t1, t1.ap, osum, osum.ap, wonorm, wonorm.ap[:, l:l + 1], rstd, rstd.ap, ALU.mult, ALU.mult)
            P.op("pool", (lambda o, a, b_: lambda e: e.tensor_tensor(out=o, in0=a, in1=b_, op=ALU.mult))(ygv[:, :, tok], h4(t1.ap), h4(sg.ap)), reads=[t1, sg], writes=[ygT])

        def inr(g):
            return 0 <= g < NG

        stages_c = [(s13, 12), (s12, 11), (s11, 10), (s10, 9), (s9, 8), (s8, 7), (s7, 6), (s6, 5), (s5, 4), (s4, 3), (s3, 2), (s2, 1), (s1, 0)]
        loads_c = [(L_aT, -2), (L_qk, 2), (L_v, 3), (L_sg, 10)]
        for s_ in range(-2, NG + 13):
            for fn, off in loads_c:
                if inr(s_ - off):
                    fn(s_ - off)
            for fn, off in stages_c:
                if inr(s_ - off):
                    fn(s_ - off)
        P.barrier()
        if stop_here(l, "C"):
            dbg = T(None)
            for gg in range(4):
                dma("sp", dbgY[gg], YT.ap.rearrange("p (g s) -> p g s", g=4)[:, gg, :], reads=[YT], writes=[dbg])
                dma("sp", dbgG[gg], ygv[:, gg, :], reads=[ygT], writes=[dbg])
            done = True
            break

        E_TR, E_WD = R1 + 22528, 39488
        wdb = T(arena[:, E_WD:E_WD + 11264].bitcast(BF16))
        wdv = wdb.ap.rearrange("p (c n) -> p c n", c=NF)
        FBLK = [(0, 6), (6, 12), (12, 17), (17, 22)]
        fblk_of = [bi for bi, (a_, b_) in enumerate(FBLK) for _ in range(a_, b_)]
        g_addr = [9792, 1600, 4672, 7232]
        u_addr = [12864, 15936, 19008, 21568]

        def blkview(addr, bi):
            w_ = (FBLK[bi][1] - FBLK[bi][0]) * 128
            return arena[:, addr:addr + 4 * w_].bitcast(BF16).rearrange("p (c n) -> p c n", c=8)

        WG = [T(None) for _ in FBLK]
        WU = [T(None) for _ in FBLK]
        WGv = [blkview(g_addr[bi], bi) for bi in range(4)]
        WUv = [blkview(u_addr[bi], bi) for bi in range(4)]
        assert g_addr[3] + 4 * 640 <= 9792 and u_addr[3] + 4 * 640 <= E_TR and u_addr[0] + 4 * 768 <= u_addr[1] and u_addr[1] <= R2
        al = Alloc(arena, R3, E_WD)
        woutb = al.bf16(8 * 1024)
        woutv = woutb.ap.rearrange("p (c n) -> p c n", c=8)
        stages = [al.f32(1024) for _ in range(4)]
        xts = [al.f32(1024) for _ in range(3)]
        t1s = [al.f32(1024), al.f32(1024)]
        dma("sp", wpost.ap, norm_mix_post[l:l + 1, :].to_broadcast([128, 1024]), writes=[wpost])
        load_weight(woutb, woutv, w_out[l], 8, 1024, stages)
        for (WT, WV, src) in ((WG, WGv, w_gate[l]), (WU, WUv, w_up[l])):
            for c in range(8):
                dma("pool", WV[0][:, c, :], src[c * 128:(c + 1) * 128, 0:768], writes=[WT[0]])
        for c in range(NF):
            dma("pool", wdv[:, c, :], w_down[l, c * 128:(c + 1) * 128, :], writes=[wdb])
        YTs = YT.ap.rearrange("p (g s) -> p g s", g=4)

        def ldx(tl):
            xt = xts[tl % 3]
            dma("sp", xt.ap, x_src[tl * 128:(tl + 1) * 128, :], reads=[X_src], writes=[xt])

        ldx(0)
        ldx(1)
        for tl in range(NT):
            xt = xts[tl % 3]
            b0 = (tl % 3) * 2
            pm = T(psum[:, b0 * 512:(b0 + 2) * 512])
            for hf in range(2):
                bank = PB[b0 + hf]
                for c in range(8):
                    src_t, src_ap = (YT, YTs[:, c, tl * 128:(tl + 1) * 128]) if c < 4 else (ygT, ygv[:, c - 4, tl * 128:(tl + 1) * 128])
                    P.op("pe", (lambda o, a, b_, st, sp: lambda e: e.matmul(o, lhsT=a, rhs=b_, start=st, stop=sp))(bank.ap, src_ap, woutv[:, c, hf * 512:(hf + 1) * 512], c == 0, c == 7), reads=[src_t, woutb], writes=[bank, pm])
            residual_out(pm, pm.ap, xt, t1s[tl % 2], xa_d[tl * 128:(tl + 1) * 128, :], D_xa, q="sp")
            for hf in range(2):
                PB[b0 + hf].b.rs.update(pm.b.rs)
            if tl + 2 < NT:
                ldx(tl + 2)
        P.barrier(skip_pool_dma=True)
        if stop_here(l, "D"):
            done = True
            break

        al = Alloc(arena, E_TR, E_WD)
        sbase = al.p
        uT = al.bf16(NF * 512)
        uTv = uT.ap.rearrange("p (f t) -> p f t", f=NF)
        hT = al.bf16(8 * 512)
        hTv = hT.ap.rearrange("p (c t) -> p c t", c=8)
        xns = [al.f32(1024), al.f32(1024)]
        xrs = [al.f32(1024), al.f32(1024)]
        hbs = [al.bf16(1024) for _ in range(4)]
        sgs = [al.bf16(512), al.bf16(512)]
        t1s = [al.f32(1024)]
        dma("sp", wpost.ap, norm_ffn_post[l:l + 1, :].to_broadcast([128, 1024]), writes=[wpost])
        def alias_stage(t_):
            n_ = T(t_.ap[:, 0:768])
            n_.b = t_.b
            return n_

        estg = [T(arena[:, E_WD + 11264:E_WD + 11264 + 768]), T(arena[:, E_WD + 11264 + 768:E_WD + 11264 + 1536]),
                alias_stage(xrs[0]), alias_stage(xrs[1]), alias_stage(t1s[0])]
        assert E_WD + 11264 + 1536 <= ARENA_COLS
        ekk = [0]

        def ld_fblk(bi):
            fa, fb = FBLK[bi]
            w_ = (fb - fa) * 128
            for (WT, WV, src) in ((WG, WGv, w_gate[l]), (WU, WUv, w_up[l])):
                for c in range(8):
                    st = estg[ekk[0] % len(estg)]
                    dma("sp", st.ap[:, 0:w_], src[c * 128:(c + 1) * 128, fa * 128:fb * 128], writes=[st])
                    cp("act" if ekk[0] % 2 else "dve", WT[bi], WV[bi][:, c, :], st, st.ap[:, 0:w_])
                    ekk[0] += 1

        gain_e = nwv[:, l, 1, :]

        def prep_norm_e(g, j):
            tl = g * 4 + j
            xt = xns[j % 2]
            dma("sp", xt.ap, xa_d[tl * 128:(tl + 1) * 128, :], reads=[D_xa], writes=[xt])
            norm_hb(xt, hbs[j])

        def prep_tr_e():
            for j in range(4):
                transpose_to_hT(hbs[j], PB[6 + j % 2], hT, hTv[:, :, j * 128:(j + 1) * 128], "act", gain_ap=gain_e)

        for j in range(4):
            prep_norm_e(0, j)
        prep_tr_e()
        ntile = 0
        for g in range(8):
            for f in range(NF):
                pg = PB[(f % 2) * 2]
                pu = PB[1 + (f % 2) * 2]
                for c in range(8):
                    mm(pg, pg.ap, WG[fblk_of[f]], WGv[fblk_of[f]][:, c, (f - FBLK[fblk_of[f]][0]) * 128:(f - FBLK[fblk_of[f]][0] + 1) * 128], hT, hTv[:, c, :], c == 0, c == 7)
                for c in range(8):
                    mm(pu, pu.ap, WU[fblk_of[f]], WUv[fblk_of[f]][:, c, (f - FBLK[fblk_of[f]][0]) * 128:(f - FBLK[fblk_of[f]][0] + 1) * 128], hT, hTv[:, c, :], c == 0, c == 7)
                sg = sgs[f % 2]
                act(sg, sg.ap, pg, pg.ap, AF.Silu)
                tt("dve", uT, uTv[:, f, :], pu, pu.ap, sg, sg.ap, ALU.mult)
                if g == 0 and f in (0, 4, 10):
                    ld_fblk({0: 1, 4: 2, 10: 3}[f])
                if g + 1 < 8 and f in (3, 8, 13, 18):
                    prep_norm_e(g + 1, (f - 3) // 5)
            if g + 1 < 8:
                prep_tr_e()
            for j in range(4):
                tl = g * 4 + j
                xr = xrs[j % 2]
                dma("sp", xr.ap, xa_d[tl * 128:(tl + 1) * 128, :], reads=[D_xa], writes=[xr])
                b0 = 4 + (ntile % 2) * 2
                ntile += 1
                pm = T(psum[:, b0 * 512:(b0 + 2) * 512])
                for hf in range(2):
                    bank = PB[b0 + hf]
                    for f in range(NF):
                        P.op("pe", (lambda o, a, b, st, sp: lambda e: e.matmul(o, lhsT=a, rhs=b, start=st, stop=sp))(bank.ap, uTv[:, f, j * 128:(j + 1) * 128], wdv[:, f, hf * 512:(hf + 1) * 512], f == 0, f == NF - 1), reads=[uT, wdb], writes=[bank, pm])
                residual_out(pm, pm.ap, xr, t1s[0], x_fin[tl * 128:(tl + 1) * 128, :], X_fin, q="sp")
                for hf in range(2):
                    PB[b0 + hf].b.rs.update(pm.b.rs)
        P.barrier()
    P.emit()
    return nc


def _consts():
    bf = ml_dtypes.bfloat16
    c = {}
    c["c_ident"] = np.eye(128, dtype=np.float32).astype(bf)
    c["c_ones"] = np.ones((128, 128), dtype=np.float32).astype(bf)
    c["c_onesf"] = np.ones((1, 128), dtype=np.float32)
    i = np.arange(128)
    ang = 2 * np.pi * np.outer(i, i) / 128.0
    c["c_dftc"] = np.concatenate([np.cos(ang), np.sin(ang)], axis=1).astype(np.float32).astype(bf)
    N = S
    s = np.arange(N // 8, dtype=np.float64)[:, None]
    kp = np.arange(N // 8, dtype=np.float64)[None, :]
    nrm = 1.0 / np.sqrt(N * 128.0)
    m = np.zeros((8, 2, N // 8, N // 8), dtype=np.float32)
    sign = [-1.0, -1.0, 1.0, 1.0]
    for rho in range(8):
        ph = 2 * np.pi * np.mod(s * (8 * kp + rho), N) / N
        m[rho, 0] = np.cos(ph) * nrm
        m[rho, 1] = np.sin(ph) * nrm * sign[rho % 4]
    c["c_dfts"] = m.astype(bf)
    j = np.arange(128)[:, None]
    t = np.arange(128)[None, :]
    lf = (j <= t).astype(np.float32)
    ub = (j >= t).astype(np.float32)
    c["c_tri"] = np.stack([lf, ub]).astype(bf)
    c["c_mask"] = np.stack([np.tile(lf, (1, 4)), np.tile(ub, (1, 4))]).astype(np.float32)
    return c


_NC = [None]


def kernel(**inputs):
    if _NC[0] is None:
        _NC[0] = build_program()
    nc = _NC[0]
    consts = _consts()
    x = np.ascontiguousarray(inputs["x"], dtype=np.float32)
    shared = {k: np.ascontiguousarray(v, dtype=np.float32) for k, v in inputs.items() if k != "x"}
    in_maps = []
    for i in range(8):
        m = dict(shared)
        m.update(consts)
        m["x"] = x[i]
        in_maps.append(m)
    res = run_bass_kernel_spmd(nc, in_maps, core_ids=list(range(8)))
    return np.stack([r["y"] for r in res.results], axis=0).astype(np.float32)
```

```python
import numpy as np
import ml_dtypes
import concourse.bass as bass
import concourse.mybir as mybir
from concourse.bass_utils import run_bass_kernel_spmd

F32 = mybir.dt.float32
BF16 = mybir.dt.bfloat16
AF = mybir.ActivationFunctionType
ALU = mybir.AluOpType

S = 4096
D = 1024
NT = S // 128
DFF = 2816
NF = DFF // 128
INC = 2080
EPS = 1e-6
DEPTH = 2
STOP_AFTER = None
DEBUG = False

ENGS = ["pe", "act", "dve", "pool", "sp"]
NDSEM = 12


class Buf:
    __slots__ = ("ws", "rs")

    def __init__(self):
        self.ws = {}
        self.rs = {}


class T:
    __slots__ = ("ap", "b")

    def __init__(self, ap):
        self.ap = ap
        self.b = Buf()


class Ins:
    __slots__ = ("eng", "fn", "deps", "signal", "cnt", "dma", "dsem", "dval")

    def __init__(self, eng, fn, dma):
        self.eng = eng
        self.fn = fn
        self.deps = []
        self.signal = False
        self.cnt = 0
        self.dma = dma
        self.dsem = None
        self.dval = 0


def _key(ins):
    return (ins.eng, ins.dsem) if ins.dma else ins.eng


class Prog:
    def __init__(self, nc):
        self.nc = nc
        self.streams = {e: [] for e in ENGS}
        self.ndma = {e: 0 for e in ENGS}
        self.last = {}

    def op(self, eng, fn, reads=(), writes=(), dma=False):
        ins = Ins(eng, fn, dma)
        if dma:
            k = self.ndma[eng]
            self.ndma[eng] += 1
            ins.dsem = k % NDSEM
        deps = ins.deps
        for t in reads:
            for w in t.b.ws.values():
                deps.append((w, True))
        for t in writes:
            b = t.b
            for w in b.ws.values():
                deps.append((w, False))
            for r in b.rs.values():
                deps.append((r, False))
        for t in reads:
            t.b.rs[_key(ins)] = ins
        for t in writes:
            b = t.b
            if b.rs:
                b.ws = {}
                b.rs = {}
            b.ws[_key(ins)] = ins
        self.streams[eng].append(ins)
        self.last[_key(ins)] = ins
        return ins

    def barrier(self, skip_pool_dma=False):
        lasts = [x for x in self.last.values() if not (skip_pool_dma and x.dma and x.eng == "pool")]
        for e in ENGS:
            ins = Ins(e, None, False)
            ins.deps = [(x, True) for x in lasts]
            self.streams[e].append(ins)

    def emit(self):
        nc = self.nc
        from contextlib import ExitStack
        es = ExitStack()
        csem = {e: es.enter_context(nc.semaphore("c_" + e)) for e in ENGS}
        dsem = {e: [es.enter_context(nc.semaphore("d_%s_%d" % (e, i))) for i in range(min(NDSEM, self.ndma[e]))] for e in ENGS}

        def relevant(e, d, raw):
            if d.dma:
                return True
            if d.eng != e:
                return True
            if e == "pe":
                return False
            return raw

        for e in ENGS:
            for ins in self.streams[e]:
                for d, raw in ins.deps:
                    if not d.dma and relevant(e, d, raw):
                        d.signal = True
        for e in ENGS:
            c = 0
            dcount = [0] * NDSEM
            for ins in self.streams[e]:
                if ins.dma:
                    dcount[ins.dsem] += 16
                    ins.dval = dcount[ins.dsem]
                elif ins.signal:
                    c += 1
                    ins.cnt = c
        block = es.enter_context(nc.Block())

        def run(e):
            def body(eng):
                waited = {}
                for ins in self.streams[e]:
                    need = {}
                    for d, raw in ins.deps:
                        if not relevant(e, d, raw):
                            continue
                        if d.dma:
                            key = ("d", d.eng, d.dsem)
                            v = d.dval
                        else:
                            key = ("c", d.eng)
                            v = d.cnt
                        if v > need.get(key, 0):
                            need[key] = v
                    if ins.dma and ins.dval > 16:
                        key = ("d", e, ins.dsem)
                        need[key] = max(need.get(key, 0), ins.dval - 16)
                    for key, v in need.items():
                        if waited.get(key, 0) >= v:
                            continue
                        waited[key] = v
                        sem = csem[key[1]] if key[0] == "c" else dsem[key[1]][key[2]]
                        eng.wait_ge(sem, v)
                    if ins.fn is None:
                        continue
                    r = ins.fn(eng)
                    if ins.dma:
                        r.then_inc(dsem[e][ins.dsem], 16)
                    elif ins.signal:
                        r.then_inc(csem[e], 1)
                for i, s in enumerate(dsem[e]):
                    tot = 16 * len([1 for x in self.streams[e] if x.dma and x.dsem == i])
                    if tot:
                        eng.wait_ge(s, tot)
            return body

        block.tensor(run("pe"))
        block.scalar(run("act"))
        block.vector(run("dve"))
        block.gpsimd(run("pool"))
        block.sync(run("sp"))
        es.close()


class Alloc:
    def __init__(self, arena, lo, hi):
        self.arena = arena
        self.lo = lo
        self.hi = hi
        self.p = lo

    def f32(self, cols, parts=128):
        a = self.p
        self.p += cols
        assert self.p <= self.hi, (self.p, self.hi)
        return T(self.arena[0:parts, a:a + cols])

    def bf16(self, cols, parts=128):
        c = (cols + 1) // 2
        a = self.p
        self.p += c
        assert self.p <= self.hi, (self.p, self.hi)
        return T(self.arena[0:parts, a:a + c].bitcast(BF16))


ARENA_COLS = 52736
C_END = 1600
R1 = 1600
R2 = 17984
R3 = 26176


def build_program():
    nc = bass.Bass("TRN2", target_bir_lowering=False)
    dt_in = lambda name, shape, dt=F32: nc.dram_tensor(name, shape, dt, kind="ExternalInput").ap()
    x_in = dt_in("x", [S, D])
    norm_mix_pre = dt_in("norm_mix_pre", [DEPTH, D])
    w_in = dt_in("w_in", [DEPTH, D, INC])
    w_af = dt_in("w_alpha_fwd", [DEPTH, 16, 256])
    b_af = dt_in("b_alpha_fwd", [DEPTH, 256])
    w_ab = dt_in("w_alpha_bwd", [DEPTH, 16, 256])
    b_ab = dt_in("b_alpha_bwd", [DEPTH, 256])
    gla_on = dt_in("gla_out_norm", [DEPTH, 128])
    w_out = dt_in("w_out", [DEPTH, D, D])
    norm_mix_post = dt_in("norm_mix_post", [DEPTH, D])
    norm_ffn_pre = dt_in("norm_ffn_pre", [DEPTH, D])
    w_gate = dt_in("w_ffn_gate", [DEPTH, D, DFF])
    w_up = dt_in("w_ffn_up", [DEPTH, D, DFF])
    w_down = dt_in("w_ffn_down", [DEPTH, DFF, D])
    norm_ffn_post = dt_in("norm_ffn_post", [DEPTH, D])
    c_ident = dt_in("c_ident", [128, 128], BF16)
    c_ones = dt_in("c_ones", [128, 128], BF16)
    c_onesf = dt_in("c_onesf", [1, 128])
    c_dftc = dt_in("c_dftc", [128, 256], BF16)
    c_dfts = dt_in("c_dfts", [8, 2, 512, 512], BF16)
    c_tri = dt_in("c_tri", [2, 128, 128], BF16)
    c_mask = dt_in("c_mask", [2, 128, 512])
    y_out = nc.dram_tensor("y", [S, D], F32, kind="ExternalOutput").ap()
    scr = lambda name, shape, dt: nc.dram_tensor(name, shape, dt, kind="ExternalOutput" if DEBUG else "Internal").ap()
    xa_d = scr("xa_d", [S, D], F32)
    xb_d = scr("xb_d", [S, D], F32)
    qT_d = scr("qT_d", [4, 64, S], BF16)
    kT_d = scr("kT_d", [4, 64, S], BF16)
    k_d = scr("k_d", [S, 256], BF16)
    v_d = scr("v_d", [S, 512], BF16)
    sgT_d = scr("sgT_d", [4, 128, S], BF16)
    aT_d = scr("aT_d", [2, 17, S], BF16)
    dbgY = scr("dbgY", [4, 128, S], BF16) if DEBUG else None
    dbgG = scr("dbgG", [4, 128, S], BF16) if DEBUG else None
    D_xa, D_xb, D_qT, D_kT, D_k, D_v, D_sgT, D_aT, D_y = [T(None) for _ in range(9)]

    arena = nc.alloc_sbuf_tensor("arena", [128, ARENA_COLS], F32)
    psum = nc.alloc_psum_tensor("psum", [128, 4096], F32)
    PB = [T(psum[:, b * 512:(b + 1) * 512]) for b in range(8)]

    P = Prog(nc)
    ca = Alloc(arena, 0, C_END)
    identb = ca.bf16(128)
    onesb = ca.bf16(128)
    onesf = ca.f32(128, parts=1)
    nw = ca.f32(32)
    wpost = ca.f32(1024)
    wonorm = ca.f32(2)
    ss_all = ca.f32(256)
    ss_ctr = [0]

    def dma(q, out_ap, in_ap, reads=(), writes=(), slow=False):
        if slow:
            return P.op(q, lambda e: e.dma_start(out=out_ap, in_=in_ap, allow_slow_non_contiguous=True), reads=reads, writes=writes, dma=True)
        return P.op(q, lambda e: e.dma_start(out=out_ap, in_=in_ap), reads=reads, writes=writes, dma=True)

    def mm(out_t, out_ap, l_t, l_ap, r_t, r_ap, start, stop):
        return P.op("pe", lambda e: e.matmul(out_ap, lhsT=l_ap, rhs=r_ap, start=start, stop=stop), reads=[l_t, r_t], writes=[out_t])

    def act(out_t, out_ap, in_t, in_ap, func, extra_reads=(), **kw):
        return P.op("act", lambda e: e.activation(out=out_ap, in_=in_ap, func=func, **kw), reads=[in_t] + list(extra_reads), writes=[out_t])

    def tt(eng, out_t, out_ap, a_t, a_ap, b_t, b_ap, op):
        return P.op(eng, lambda e: e.tensor_tensor(out=out_ap, in0=a_ap, in1=b_ap, op=op), reads=[a_t, b_t], writes=[out_t])

    def ts(eng, out_t, out_ap, a_t, a_ap, s1, s2, op0, op1=None, extra_reads=()):
        if op1 is None:
            return P.op(eng, lambda e: e.tensor_scalar(out=out_ap, in0=a_ap, scalar1=s1, scalar2=None, op0=op0), reads=[a_t] + list(extra_reads), writes=[out_t])
        return P.op(eng, lambda e: e.tensor_scalar(out=out_ap, in0=a_ap, scalar1=s1, scalar2=s2, op0=op0, op1=op1), reads=[a_t] + list(extra_reads), writes=[out_t])

    def stt(eng, out_t, out_ap, a_t, a_ap, sc_t, sc_ap, b_t, b_ap, op0, op1):
        return P.op(eng, lambda e: e.scalar_tensor_tensor(out=out_ap, in0=a_ap, scalar=sc_ap, in1=b_ap, op0=op0, op1=op1), reads=[a_t, sc_t, b_t], writes=[out_t])

    def cp(eng, out_t, out_ap, in_t, in_ap):
        if eng == "act":
            return P.op("act", lambda e: e.copy(out=out_ap, in_=in_ap), reads=[in_t], writes=[out_t])
        return P.op(eng, lambda e: e.tensor_copy(out=out_ap, in_=in_ap), reads=[in_t], writes=[out_t])

    for a_ in range(2):
        dma("sp", aT_d[a_, 16:17, :], c_ones[0:32, :].rearrange("(o a) b -> o (a b)", o=1), writes=[D_aT])
    P.op("pool", lambda e: e.memset(ss_all.ap, 0.0), writes=[ss_all])
    dma("sp", identb.ap, c_ident, writes=[identb])
    dma("sp", onesb.ap, c_ones, writes=[onesb])
    dma("sp", onesf.ap, c_onesf, writes=[onesf])
    nwv = nw.ap.rearrange("p (l k c) -> p l k c", l=2, k=2)
    for l in range(DEPTH):
        dma("sp", nwv[:, l, 0, :], norm_mix_pre[l].rearrange("(c p) -> p c", p=128), writes=[nw], slow=True)
        dma("sp", nwv[:, l, 1, :], norm_ffn_pre[l].rearrange("(c p) -> p c", p=128), writes=[nw], slow=True)
        dma("sp", wonorm.ap[:, l:l + 1], gla_on[l].rearrange("(p o) -> p o", o=1), writes=[wonorm], slow=True)

    def load_weight(dst_t, dst_ap3, src, nchunk, ncols, stages, scale_ap=None, k0=0):
        engs = ["dve", "act"]
        for c in range(nchunk):
            st = stages[(k0 + c) % len(stages)]
            dma("sp", st.ap[:, 0:ncols], src[c * 128:(c + 1) * 128, :], writes=[st])
            eng = engs[(k0 + c) % 2]
            if scale_ap is not None:
                sc = scale_ap[:, c:c + 1]
                if eng == "act":
                    P.op("act", (lambda o, i, s_: lambda e: e.activation(out=o, in_=i, func=AF.Copy, scale=s_))(dst_ap3[:, c, :], st.ap[:, 0:ncols], sc), reads=[st, nw], writes=[dst_t])
                else:
                    ts(eng, dst_t, dst_ap3[:, c, :], st, st.ap[:, 0:ncols], sc, None, ALU.mult, extra_reads=[nw])
            else:
                cp(eng, dst_t, dst_ap3[:, c, :], st, st.ap[:, 0:ncols])

    def rmsnorm_stats(src_t, src_ap, junk, junk_ap):
        c = ss_ctr[0]
        ss_ctr[0] += 1
        col = ss_all.ap[:, c % 256:c % 256 + 1]
        sq = T(None)
        P.op("act", lambda e: e.activation(out=junk_ap, in_=src_ap, func=AF.Square, accum_out=col), reads=[src_t, ss_all], writes=[junk, sq])
        P.op("act", lambda e: e.activation(out=col, in_=col, func=AF.Sqrt, scale=1.0 / D, bias=EPS), reads=[sq], writes=[sq])
        P.op("dve", lambda e: e.reciprocal(out=col, in_=col), reads=[sq], writes=[sq])
        return sq, col

    def norm_hb(xt, hb):
        sq, col = rmsnorm_stats(xt, xt.ap, hb, hb.ap)
        ts("dve", hb, hb.ap, xt, xt.ap, col, None, ALU.mult, extra_reads=[sq])

    def transpose_to_hT(hb, tp, hT, hT_view, ceng, gain_ap=None):
        tpb = tp.ap.bitcast(BF16)
        for c in range(8):
            P.op("pe", (lambda o, i: lambda e: e.transpose(out=o, in_=i, identity=identb.ap))(tpb[:, c * 128:(c + 1) * 128], hb.ap[:, c * 128:(c + 1) * 128]), reads=[hb, identb], writes=[tp])
        if gain_ap is None:
            cp(ceng, hT, hT_view, tp, tpb.rearrange("p (c t) -> p c t", c=8))
        else:
            P.op("dve", (lambda o, a, b_: lambda e: e.tensor_tensor(out=o, in0=a, in1=b_, op=ALU.mult))(hT_view, tpb.rearrange("p (c t) -> p c t", c=8), gain_ap.unsqueeze(2).to_broadcast([128, 8, 128])), reads=[tp, nw], writes=[hT])

    def residual_out(m_t, m_ap, xt, t1, dst_ap, dst_T, q="pool"):
        sq, col = rmsnorm_stats(m_t, m_ap, t1, t1.ap)
        stt("dve", t1, t1.ap, m_t, m_ap, sq, col, wpost, wpost.ap, ALU.mult, ALU.mult)
        tt("dve", xt, xt.ap, xt, xt.ap, t1, t1.ap, ALU.add)
        dma(q, dst_ap, xt.ap, reads=[xt], writes=[dst_T])

    def stop_here(l, ph):
        return STOP_AFTER is not None and STOP_AFTER == (l, ph)

    def ring(al, n, cols, dt=F32, parts=128):
        return [al.f32(cols, parts) if dt == F32 else al.bf16(cols, parts) for _ in range(n)]

    done = False
    for l in range(DEPTH):
        if done:
            break
        x_src, X_src = (x_in, T(None)) if l == 0 else (xb_d, D_xb)
        x_fin, X_fin = (xb_d, D_xb) if l == 0 else (y_out, D_y)
        al = Alloc(arena, R2, ARENA_COLS)
        AB = T(arena[:, R1:R1 + 16384].bitcast(BF16))
        ABv = AB.ap.rearrange("p (t c) -> p t c", c=1024)
        winb_all = al.bf16(8 * INC)
        winv = winb_all.ap.rearrange("p (c n) -> p c n", c=8)
        WBLK = [(0, 512), (512, 1024), (1024, 1536), (1536, 2048), (2048, 2080)]
        WB = [T(winb_all.ap) for _ in WBLK]
        stages = [al.f32(544) for _ in range(4)]
        dftc = al.bf16(256)
        xts = [al.f32(1024) for _ in range(4)]
        hbs = [al.bf16(1024) for _ in range(4)]
        hTs = [al.bf16(8 * 512), al.bf16(8 * 512)]
        fpT = al.bf16(4 * 512)
        qst = al.bf16(2 * 512)
        kst = al.bf16(2 * 512)
        ktok = al.bf16(4 * 256)
        vtok = al.bf16(4 * 512)
        sgst = al.bf16(4 * 512)
        ast = al.bf16(512, parts=32)
        tmp = [al.f32(512) for _ in range(8)]
        dma("sp", dftc.ap, c_dftc, writes=[dftc])
        kkc = [0]

        def ld_wblk(bi):
            lo_, hi_ = WBLK[bi]
            for c in range(8):
                kk = kkc[0]
                st = stages[kk % 4]
                dma("sp", st.ap[:, 0:hi_ - lo_], w_in[l, c * 128:(c + 1) * 128, lo_:hi_], writes=[st])
                sc = nwv[:, l, 0, c:c + 1]
                if kk % 2:
                    P.op("act", (lambda o, i, s_: lambda e: e.activation(out=o, in_=i, func=AF.Copy, scale=s_))(winv[:, c, lo_:hi_], st.ap[:, 0:hi_ - lo_], sc), reads=[st, nw], writes=[WB[bi]])
                else:
                    ts("dve", WB[bi], winv[:, c, lo_:hi_], st, st.ap[:, 0:hi_ - lo_], sc, None, ALU.mult, extra_reads=[nw])
                kkc[0] += 1

        rot = [0]

        dftrot = [0]
        ABS = [T(AB.ap) for _ in range(32)]

        def nextbank(lo=2, n=3):
            b = PB[lo + rot[0] % n]
            rot[0] += 1
            return b

        def prep_norm(t):
            for j in range(4):
                tl = t + 8 * j
                xt = xts[j]
                dma("sp", xt.ap, x_src[tl * 128:(tl + 1) * 128, :], reads=[X_src], writes=[xt])
                norm_hb(xt, hbs[j])

        def prep_tr(t):
            hT = hTs[t % 2]
            hTv = hT.ap.rearrange("p (c t) -> p c t", c=8)
            for j in range(4):
                transpose_to_hT(hbs[j], PB[j % 2], hT, hTv[:, :, j * 128:(j + 1) * 128], "act")

        prep_norm(0)
        ld_wblk(0)
        prep_tr(0)
        ld_wblk(1)
        for t in range(8):
            hT = hTs[t % 2]
            hTv = hT.ap.rearrange("p (c t) -> p c t", c=8)
            if t + 1 < 8:
                prep_norm(t + 1)
            fpv = fpT.ap.rearrange("p (g t) -> p g t", g=4)
            for gg in range(4):
                pb = nextbank()
                for c in range(8):
                    mm(pb, pb.ap, WB[0], winv[:, c, gg * 128:(gg + 1) * 128], hT, hTv[:, c, :], c == 0, c == 7)
                cp("act" if gg % 2 else "dve", fpT, fpv[:, gg, :], pb, pb.ap)
            if t == 0:
                ld_wblk(2)
            for (off, stt_, dst_d, DT, scale) in ((512, qst, qT_d, D_qT, 0.125), (768, kst, kT_d, D_kT, 1.0)):
                sv = stt_.ap.rearrange("p (h t) -> p h t", h=2)
                for hp in range(2):
                    pb = nextbank()
                    for c in range(8):
                        mm(pb, pb.ap, WB[1], winv[:, c, off + hp * 128:off + (hp + 1) * 128], hT, hTv[:, c, :], c == 0, c == 7)
                    if hp % 2:
                        P.op("act", (lambda o, i, s_: lambda e: e.activation(out=o, in_=i, func=AF.Copy, scale=s_))(sv[:, hp, :], pb.ap, scale), reads=[pb], writes=[stt_])
                    else:
                        ts("dve", stt_, sv[:, hp, :], pb, pb.ap, scale, None, ALU.mult)
                for h in range(4):
                    dma("pool", dst_d[h].rearrange("d (j t p) -> d j t p", j=4, t=8)[:, :, t, :], sv[(h % 2) * 64:(h % 2) * 64 + 64, h // 2, :].rearrange("p (j q) -> p j q", j=4), reads=[stt_], writes=[DT])
            if t == 0:
                ld_wblk(3)
                ld_wblk(4)
            for j in range(4):
                tl = t + 8 * j
                for hh in range(2):
                    half = PB[5 + dftrot[0] % 3]
                    dftrot[0] += 1
                    for g2 in range(2):
                        gg = hh * 2 + g2
                        mm(half, half.ap[:, g2 * 256:g2 * 256 + 256], fpT, fpv[:, gg, j * 128:(j + 1) * 128], dftc, dftc.ap, True, True)
                    src = half.ap.rearrange("p (g a c) -> p a g c", g=2, a=2)
                    dst = ABv[:, tl, :].rearrange("p (a g c) -> p a g c", a=2, g=4)[:, :, hh * 2:hh * 2 + 2, :]
                    cp("act" if hh else "dve", ABS[tl], dst, half, src)
            A = [ABv[:, t + 8 * j, 0:512] for j in range(4)]
            Bq = [ABv[:, t + 8 * j, 512:1024] for j in range(4)]
            S_ = [ABS[t + 8 * j] for j in range(4)]
            eA, fA, gA, hA, eB, fB, gB, hB = tmp
            bfly = []

            def bf(o_t, o_ap, a_t, a_ap, b_t, b_ap, op):
                bfly.append(lambda: tt("dve", o_t, o_ap, a_t, a_ap, b_t, b_ap, op))

            def bf_stt(o_t, o_ap, p_t, p_ap, cst, b_t, b_ap):
                bfly.append(lambda: P.op("dve", lambda e: e.scalar_tensor_tensor(out=o_ap, in0=p_ap, scalar=cst, in1=b_ap, op0=ALU.mult, op1=ALU.add), reads=[p_t, b_t], writes=[o_t]))

            def bf_cp(o_t, o_ap, i_t, i_ap):
                bfly.append(lambda: cp("act", o_t, o_ap, i_t, i_ap))

            bf(eA, eA.ap, S_[0], A[0], S_[2], A[2], ALU.add)
            bf(fA, fA.ap, S_[0], A[0], S_[2], A[2], ALU.subtract)
            bf(gA, gA.ap, S_[1], A[1], S_[3], A[3], ALU.add)
            bf(hA, hA.ap, S_[1], A[1], S_[3], A[3], ALU.subtract)
            bf(eB, eB.ap, S_[0], Bq[0], S_[2], Bq[2], ALU.add)
            bf(fB, fB.ap, S_[0], Bq[0], S_[2], Bq[2], ALU.subtract)
            bf(gB, gB.ap, S_[1], Bq[1], S_[3], Bq[3], ALU.add)
            bf(hB, hB.ap, S_[1], Bq[1], S_[3], Bq[3], ALU.subtract)
            bf(S_[0], A[0], eA, eA.ap, gA, gA.ap, ALU.add)
            bf(S_[0], Bq[0], eB, eB.ap, gB, gB.ap, ALU.add)
            bf(S_[1], A[1], fA, fA.ap, hB, hB.ap, ALU.subtract)
            bf(S_[1], Bq[1], fB, fB.ap, hA, hA.ap, ALU.add)
            bf(S_[2], A[2], eA, eA.ap, gA, gA.ap, ALU.subtract)
            bf(S_[2], Bq[2], gB, gB.ap, eB, eB.ap, ALU.subtract)
            bf(S_[3], A[3], fA, fA.ap, hB, hB.ap, ALU.add)
            bf(S_[3], Bq[3], hA, hA.ap, fB, fB.ap, ALU.subtract)

            if t >= 4:
                t2 = t - 4
                c8 = float(np.sqrt(0.5))
                for r_ in range(4):
                    lo_s, hi_s = t2 + 8 * r_, t2 + 4 + 8 * r_
                    Lt, Ht = ABS[lo_s], ABS[hi_s]
                    aL, bL = ABv[:, lo_s, 0:512], ABv[:, lo_s, 512:1024]
                    aH, bH = ABv[:, hi_s, 0:512], ABv[:, hi_s, 512:1024]
                    u1, u2 = tmp[(2 * r_) % 8], tmp[(2 * r_ + 1) % 8]
                    if r_ == 0:
                        bf(u1, u1.ap, Lt, aL, Ht, aH, ALU.subtract)
                        bf(u2, u2.ap, Lt, bL, Ht, bH, ALU.subtract)
                        bf(Lt, aL, Lt, aL, Ht, aH, ALU.add)
                        bf(Lt, bL, Lt, bL, Ht, bH, ALU.add)
                        bf_cp(Ht, aH, u1, u1.ap)
                        bf_cp(Ht, bH, u2, u2.ap)
                    elif r_ == 2:
                        bf(u1, u1.ap, Lt, aL, Ht, bH, ALU.subtract)
                        bf(u2, u2.ap, Lt, bL, Ht, aH, ALU.add)
                        bf(Lt, aL, Lt, aL, Ht, bH, ALU.add)
                        bf(Lt, bL, Lt, bL, Ht, aH, ALU.subtract)
                        bf_cp(Ht, aH, u1, u1.ap)
                        bf_cp(Ht, bH, u2, u2.ap)
                    else:
                        sg_ = 1.0 if r_ == 1 else -1.0
                        bf(u1, u1.ap, Ht, aH, Ht, bH, ALU.subtract)
                        bf(u2, u2.ap, Ht, aH, Ht, bH, ALU.add)
                        bf_stt(Ht, aH, u1, u1.ap, -sg_ * c8, Lt, aL)
                        bf_stt(Ht, bH, u2, u2.ap, -sg_ * c8, Lt, bL)
                        bf_stt(Lt, aL, u1, u1.ap, sg_ * c8, Lt, aL)
                        bf_stt(Lt, bL, u2, u2.ap, sg_ * c8, Lt, bL)

            def emit_bf(n_):
                for _ in range(n_):
                    if bfly:
                        bfly.pop(0)()

            if t + 1 < 8:
                prep_tr(t + 1)
            kv = ktok.ap.rearrange("p (j n) -> p j n", j=4)
            vv = vtok.ap.rearrange("p (j n) -> p j n", j=4)
            ksv = kst.ap.rearrange("p (h t) -> p h t", h=2)
            pbk = nextbank()
            pbk16 = pbk.ap.bitcast(BF16).rearrange("p (j h n) -> p j h n", j=4, h=2)
            for j in range(4):
                for hp in range(2):
                    P.op("pe", (lambda o, i: lambda e: e.transpose(out=o, in_=i, identity=identb.ap))(pbk16[:, j, hp, :], ksv[:, hp, j * 128:(j + 1) * 128]), reads=[kst, identb], writes=[pbk])
            cp("act", ktok, ktok.ap, pbk, pbk.ap.bitcast(BF16))
            for j in range(4):
                pb = nextbank()
                for c in range(8):
                    mm(pb, pb.ap, hT, hTv[:, c, j * 128:(j + 1) * 128], WB[2], winv[:, c, 1024:1536], c == 0, c == 7)
                cp("act", vtok, vv[:, j, :], pb, pb.ap)
                emit_bf(6)
            dma("pool", k_d.rearrange("(j t p) n -> p j t n", j=4, t=8)[:, :, t, :], kv, reads=[ktok], writes=[D_k])
            dma("pool", v_d.rearrange("(j t p) n -> p j t n", j=4, t=8)[:, :, t, :], vv, reads=[vtok], writes=[D_v])
            sgv = sgst.ap.rearrange("p (h t) -> p h t", h=4)
            for h in range(4):
                pb = nextbank()
                for c in range(8):
                    mm(pb, pb.ap, WB[3], winv[:, c, 1536 + h * 128:1536 + (h + 1) * 128], hT, hTv[:, c, :], c == 0, c == 7)
                act(sgst, sgv[:, h, :], pb, pb.ap, AF.Silu)
                emit_bf(3)
            for h in range(4):
                dma("pool", sgT_d[h].rearrange("d (j t p) -> d j t p", j=4, t=8)[:, :, t, :], sgv[:, h, :].rearrange("p (j q) -> p j q", j=4), reads=[sgst], writes=[D_sgT])
            pb = nextbank()
            for c in range(8):
                mm(pb, pb.ap[0:32, :], WB[4], winv[:, c, 2048:2080], hT, hTv[:, c, :], c == 0, c == 7)
            cp("dve", ast, ast.ap, pb, pb.ap[0:32, :])
            emit_bf(64)
            for a in range(2):
                dma("pool", aT_d[a, 0:16, :].rearrange("r (j t p) -> r j t p", j=4, t=8)[:, :, t, :], ast.ap[a * 16:(a + 1) * 16, :].rearrange("p (j q) -> p j q", j=4), reads=[ast], writes=[D_aT])
        MPRE = 48640
        assert al.p <= MPRE
        mat_pre = [T(arena[:, MPRE + i * 1024:MPRE + (i + 1) * 1024].bitcast(BF16)) for i in range(2)]
        for ab in range(2):
            dma("sp", mat_pre[ab].ap.rearrange("p (t k) -> p t k", t=4), c_dfts[0, ab].rearrange("(t p) k -> p t k", p=128), writes=[mat_pre[ab]])
        P.barrier()
        if stop_here(l, "A"):
            done = True
            break

        al = Alloc(arena, R3, MPRE - 2048)
        tri = al.bf16(256)
        mask = al.f32(1024)
        walpha = al.bf16(512, parts=17)
        dma("sp", tri.ap.rearrange("p (a t) -> p a t", a=2), c_tri.rearrange("a p t -> p a t"), writes=[tri])
        dma("sp", mask.ap.rearrange("p (a t) -> p a t", a=2), c_mask.rearrange("a p t -> p a t"), writes=[mask])
        dma("pool", walpha.ap[0:16, 0:256], w_af[l], writes=[walpha])
        dma("pool", walpha.ap[0:16, 256:512], w_ab[l], writes=[walpha])
        dma("pool", walpha.ap[16:17, 0:256], b_af[l:l + 1, :], writes=[walpha])
        dma("pool", walpha.ap[16:17, 256:512], b_ab[l:l + 1, :], writes=[walpha])
        c_alloc_next = al.p
        YT = T(arena[:, R2:R2 + 8192].bitcast(BF16))
        YTv = YT.ap.rearrange("p (g k r) -> p g k r", g=4, r=8)
        mats = mat_pre + [T(arena[:, MPRE - 2048 + i * 1024:MPRE - 2048 + (i + 1) * 1024].bitcast(BF16)) for i in range(2)]
        rot[0] = 0
        for rho in range(8):
            r_, q_ = rho % 4, rho // 4
            mre, mim = mats[(rho % 2) * 2], mats[(rho % 2) * 2 + 1]
            if rho > 0:
                for ab, mt in ((0, mre), (1, mim)):
                    dma("sp", mt.ap.rearrange("p (t k) -> p t k", t=4), c_dfts[rho, ab].rearrange("(t p) k -> p t k", p=128), writes=[mt])
            mrev = mre.ap.rearrange("p (t k) -> p t k", t=4)
            mimv = mim.ap.rearrange("p (t k) -> p t k", t=4)
            for gg in range(4):
                pb = nextbank(0, 8)
                for t2 in range(4):
                    sl_ = t2 + 4 * q_ + 8 * r_
                    mm(pb, pb.ap, ABS[sl_], ABv[:, sl_, gg * 128:(gg + 1) * 128], mre, mrev[:, t2, :], t2 == 0, False)
                    mm(pb, pb.ap, ABS[sl_], ABv[:, sl_, 512 + gg * 128:512 + (gg + 1) * 128], mim, mimv[:, t2, :], False, t2 == 3)
                cp("act" if gg % 2 else "dve", YT, YTv[:, gg, :, rho], pb, pb.ap)
        P.barrier()
        if stop_here(l, "B"):
            done = True
            break

        al = Alloc(arena, c_alloc_next, MPRE - 2048)
        ygT = T(arena[:, R1:R1 + 8192].bitcast(BF16))
        OT = T(arena[:, R1 + 8192:R1 + 16384].bitcast(BF16))
        ygv = ygT.ap.rearrange("p (h s) -> p h s", h=4)
        OTv = OT.ap.rearrange("p (h s) -> p h s", h=4)
        R_aT = ring(al, 4, 128, BF16, 17)
        R_kt = ring(al, 4, 256, BF16)
        R_qT = ring(al, 4, 256, BF16)
        R_kT = ring(al, 4, 256, BF16)
        R_v = ring(al, 7, 512, BF16)
        R_sg = ring(al, 3, 512, BF16)
        R_ex = ring(al, 2, 256)
        R_lap = ring(al, 3, 256, BF16)
        R_Ek = ring(al, 3, 256)
        R_ke = ring(al, 3, 256, BF16)
        R_EqT = ring(al, 5, 256)
        R_EkT = ring(al, 3, 256)
        R_qeT = ring(al, 6, 256, BF16)
        R_keT = ring(al, 3, 256, BF16)
        R_att = ring(al, 4, 512, BF16)
        R_prev = ring(al, 4, 256, BF16)
        R_tmps = ring(al, 2, 256)
        R_osum = ring(al, 5, 512)
        R_sq = ring(al, 3, 512, BF16)
        R_rstd = ring(al, 3, 512)
        R_t1 = ring(al, 2, 512)
        state = al.f32(256)
        NG = 2 * NT

        def r2(ap):
            return ap.rearrange("p (r t) -> p r t", r=2)

        def gi_info(gi):
            dr = gi // NT
            ci = gi % NT
            n = ci if dr == 0 else NT - 1 - ci
            return dr, n, slice(n * 128, (n + 1) * 128)

        def pick(rg, gi):
            return rg[gi % len(rg)]

        def h4(ap):
            return ap.rearrange("p (h t) -> p h t", h=4)

        def pl_of(gi):
            return PB[0], PB[0].ap[:, 0:256]

        def pc_of(gi):
            return PB[1], PB[1].ap[:, 0:256]

        PCT = PB[1].ap[:, 256:512]

        def L_aT(gi):
            dr, n, tok = gi_info(gi)
            b_ = pick(R_aT, gi)
            dma("sp", b_.ap, aT_d[dr, :, tok], reads=[D_aT], writes=[b_])

        def L_qk(gi):
            dr, n, tok = gi_info(gi)
            b_ = pick(R_qT, gi)
            dma("sp", r2(b_.ap), qT_d.rearrange("(r e) d t -> (e d) r t", e=2)[:, :, tok], reads=[D_qT], writes=[b_])
            b_ = pick(R_kT, gi)
            dma("sp", r2(b_.ap), kT_d.rearrange("(r e) d t -> (e d) r t", e=2)[:, :, tok], reads=[D_kT], writes=[b_])
            b_ = pick(R_kt, gi)
            dma("sp", b_.ap, k_d[tok, :], reads=[D_k], writes=[b_])

        def L_v(gi):
            dr, n, tok = gi_info(gi)
            b_ = pick(R_v, gi)
            dma("sp", b_.ap, v_d[tok, :], reads=[D_v], writes=[b_])

        def L_sg(gi):
            dr, n, tok = gi_info(gi)
            if dr == 1:
                b_ = pick(R_sg, gi)
                dma("sp", h4(b_.ap), sgT_d[:, :, tok].rearrange("h d t -> d h t"), reads=[D_sgT], writes=[b_])

        def s1(gi):
            dr, n, tok = gi_info(gi)
            aT = pick(R_aT, gi)
            plT, pl = pl_of(gi)
            mm(plT, pl, aT, aT.ap, walpha, walpha.ap[:, dr * 256:(dr + 1) * 256], True, True)

        def s2(gi):
            ex, lap = pick(R_ex, gi), pick(R_lap, gi)
            plT, pl = pl_of(gi)
            act(ex, ex.ap, plT, pl, AF.Exp, scale=-1.0)
            act(lap, lap.ap, ex, ex.ap, AF.Ln, bias=1.0, scale=1.0)

        def s3(gi):
            dr, n, tok = gi_info(gi)
            lap = pick(R_lap, gi)
            triap = tri.ap[:, dr * 128:(dr + 1) * 128]
            pcT, pc = pc_of(gi)
            mm(pcT, pc, tri, triap, lap, lap.ap, True, True)
            for pr in range(2):
                mm(PB[1], PCT[:, pr * 128:(pr + 1) * 128], lap, lap.ap[:, pr * 128:(pr + 1) * 128], tri, triap, True, True)

        def s4(gi):
            Ek, EqT, EkT = pick(R_Ek, gi), pick(R_EqT, gi), pick(R_EkT, gi)
            pcT, pc = pc_of(gi)
            act(EqT, EqT.ap, PB[1], PCT, AF.Exp, scale=-1.0 / 16)
            act(EkT, EkT.ap, PB[1], PCT, AF.Exp, scale=1.0 / 16)
            act(Ek, Ek.ap, pcT, pc, AF.Exp, scale=1.0 / 16)

        def s5(gi):
            kt, Ek, ke = pick(R_kt, gi), pick(R_Ek, gi), pick(R_ke, gi)
            qT, kT, EqT, EkT, qeT, keT = pick(R_qT, gi), pick(R_kT, gi), pick(R_EqT, gi), pick(R_EkT, gi), pick(R_qeT, gi), pick(R_keT, gi)
            tt("dve", qeT, qeT.ap, qT, qT.ap, EqT, EqT.ap, ALU.mult)
            tt("pool", keT, keT.ap, kT, kT.ap, EkT, EkT.ap, ALU.mult)
            tt("pool", ke, ke.ap, kt, kt.ap, Ek, Ek.ap, ALU.mult)

        def s6(gi):
            keT, qeT, ke, v = pick(R_keT, gi), pick(R_qeT, gi), pick(R_ke, gi), pick(R_v, gi)
            for h in range(4):
                pr, e_ = h // 2, h % 2
                ps_, cs_ = slice(e_ * 64, (e_ + 1) * 64), slice(pr * 128, (pr + 1) * 128)
                pa = PB[2 + e_]
                mm(pa, pa.ap[:, cs_], keT, keT.ap[ps_, cs_], qeT, qeT.ap[ps_, cs_], True, True)
            pk = PB[4]
            for h in range(4):
                pr = h // 2
                mm(pk, pk.ap[:, h * 128:(h + 1) * 128], ke, ke.ap[:, pr * 128:(pr + 1) * 128], v, v.ap[:, h * 128:(h + 1) * 128], True, True)

        def s7(gi):
            dr, n, tok = gi_info(gi)
            if gi % NT == 0:
                P.op("pool", lambda e: e.memset(state.ap, 0.0), writes=[state])
                pv0 = pick(R_prev, gi)
                P.op("pool", (lambda o: lambda e: e.memset(o, 0.0))(pv0.ap), writes=[pv0])
            dcol = 127 if dr == 0 else 0
            att, EqT, tmps = pick(R_att, gi), pick(R_EqT, gi), pick(R_tmps, gi)
            pk = PB[4]
            for e_ in range(2):
                ps_ = slice(e_ * 64, (e_ + 1) * 64)
                P.op("dve", (lambda o, a, b_: lambda e: e.tensor_tensor(out=o, in0=a, in1=b_, op=ALU.add))(r2(tmps.ap)[ps_], r2(state.ap)[ps_], pk.ap[ps_, :].rearrange("p (r f t) -> p r f t", r=2, f=2)[:, :, e_, :]), reads=[state, pk], writes=[tmps])
            for e_ in range(2):
                pa = PB[2 + e_]
                P.op("dve", (lambda o, a, b_: lambda e: e.tensor_tensor(out=o, in0=a, in1=b_, op=ALU.mult))(att.ap.rearrange("p (r f t) -> p r f t", r=2, f=2)[:, :, e_, :], r2(pa.ap[:, 0:256]), r2(mask.ap[:, dr * 512:dr * 512 + 256])), reads=[pa, mask], writes=[att])
            P.op("dve", (lambda o, a, b_: lambda e: e.tensor_tensor(out=o, in0=a, in1=b_, op=ALU.mult))(r2(state.ap), r2(tmps.ap), r2(EqT.ap)[:, :, dcol:dcol + 1].to_broadcast([128, 2, 128])), reads=[tmps, EqT], writes=[state])

        def s8(gi):
            pvn = pick(R_prev, gi + 1)
            cp("act", pvn, pvn.ap, state, state.ap)

        def s9(gi):
            v, att, pv, qeT = pick(R_v, gi), pick(R_att, gi), pick(R_prev, gi), pick(R_qeT, gi)
            po = PB[5 + gi % 2]
            for h in range(4):
                hs = slice(h * 128, (h + 1) * 128)
                pr, e_ = h // 2, h % 2
                ps_, cs_ = slice(e_ * 64, (e_ + 1) * 64), slice(pr * 128, (pr + 1) * 128)
                mm(po, po.ap[:, hs], v, v.ap[:, hs], att, att.ap[:, hs], True, False)
                mm(po, po.ap[:, hs], pv, pv.ap[ps_, cs_], qeT, qeT.ap[ps_, cs_], False, True)

        def s10(gi):
            dr, n, tok = gi_info(gi)
            po = PB[5 + gi % 2]
            if dr == 0:
                cp("act", OT, OTv[:, :, tok], po, h4(po.ap))
                return
            osum, sq = pick(R_osum, gi), pick(R_sq, gi)
            P.op("dve", (lambda o, a, b_: lambda e: e.tensor_tensor(out=o, in0=a, in1=b_, op=ALU.add))(h4(osum.ap), h4(po.ap), OTv[:, :, tok]), reads=[po, OT], writes=[osum])
            act(sq, sq.ap, osum, osum.ap, AF.Square)

        def s11(gi):
            dr, n, tok = gi_info(gi)
            if dr == 0:
                return
            sq = pick(R_sq, gi)
            pn = PB[7]
            mm(pn, pn.ap, onesb, onesb.ap, sq, sq.ap, True, True)

        def s12(gi):
            dr, n, tok = gi_info(gi)
            if dr == 0:
                return
            rstd = pick(R_rstd, gi)
            pn = PB[7]
            act(rstd, rstd.ap, pn, pn.ap, AF.Ln, scale=1.0 / 128, bias=EPS)
            act(rstd, rstd.ap, rstd, rstd.ap, AF.Exp, scale=-0.5)

        def s13(gi):
            dr, n, tok = gi_info(gi)
            if dr == 0:
                return
            osum, rstd, t1, sg = pick(R_osum, gi), pick(R_rstd, gi), pick(R_t1, gi), pick(R_sg, gi)
            stt("dve", t1, t1.ap, osum, osum.ap, wonorm, wonorm.ap[:, l:l + 1], rstd, rstd.ap, ALU.mult, ALU.mult)
            P.op("pool", (lambda o, a, b_: lambda e: e.tensor_tensor(out=o, in0=a, in1=b_, op=ALU.mult))(ygv[:, :, tok], h4(t1.ap), h4(sg.ap)), reads=[t1, sg], writes=[ygT])

        def inr(g):
            return 0 <= g < NG

        stages_c = [(s13, 12), (s12, 11), (s11, 10), (s10, 9), (s9, 8), (s8, 7), (s7, 6), (s6, 5), (s5, 4), (s4, 3), (s3, 2), (s2, 1), (s1, 0)]
        loads_c = [(L_aT, -2), (L_qk, 2), (L_v, 3), (L_sg, 10)]
        for s_ in range(-2, NG + 13):
            for fn, off in loads_c:
                if inr(s_ - off):
                    fn(s_ - off)
            for fn, off in stages_c:
                if inr(s_ - off):
                    fn(s_ - off)
        P.barrier()
        if stop_here(l, "C"):
            dbg = T(None)
            for gg in range(4):
                dma("sp", dbgY[gg], YT.ap.rearrange("p (g s) -> p g s", g=4)[:, gg, :], reads=[YT], writes=[dbg])
                dma("sp", dbgG[gg], ygv[:, gg, :], reads=[ygT], writes=[dbg])
            done = True
            break

        E_TR, E_WD = R1 + 22528, 39488
        wdb = T(arena[:, E_WD:E_WD + 11264].bitcast(BF16))
        wdv = wdb.ap.rearrange("p (c n) -> p c n", c=NF)
        FBLK = [(0, 6), (6, 12), (12, 17), (17, 22)]
        fblk_of = [bi for bi, (a_, b_) in enumerate(FBLK) for _ in range(a_, b_)]
        g_addr = [9792, 1600, 4672, 7232]
        u_addr = [12864, 15936, 19008, 21568]

        def blkview(addr, bi):
            w_ = (FBLK[bi][1] - FBLK[bi][0]) * 128
            return arena[:, addr:addr + 4 * w_].bitcast(BF16).rearrange("p (c n) -> p c n", c=8)

        WG = [T(None) for _ in FBLK]
        WU = [T(None) for _ in FBLK]
        WGv = [blkview(g_addr[bi], bi) for bi in range(4)]
        WUv = [blkview(u_addr[bi], bi) for bi in range(4)]
        assert g_addr[3] + 4 * 640 <= 9792 and u_addr[3] + 4 * 640 <= E_TR and u_addr[0] + 4 * 768 <= u_addr[1] and u_addr[1] <= R2
        al = Alloc(arena, R3, E_WD)
        woutb = al.bf16(8 * 1024)
        woutv = woutb.ap.rearrange("p (c n) -> p c n", c=8)
        stages = [al.f32(1024) for _ in range(4)]
        xts = [al.f32(1024) for _ in range(3)]
        t1s = [al.f32(1024), al.f32(1024)]
        dma("sp", wpost.ap, norm_mix_post[l:l + 1, :].to_broadcast([128, 1024]), writes=[wpost])
        load_weight(woutb, woutv, w_out[l], 8, 1024, stages)
        for (WT, WV, src) in ((WG, WGv, w_gate[l]), (WU, WUv, w_up[l])):
            for c in range(8):
                dma("pool", WV[0][:, c, :], src[c * 128:(c + 1) * 128, 0:768], writes=[WT[0]])
        for c in range(NF):
            dma("pool", wdv[:, c, :], w_down[l, c * 128:(c + 1) * 128, :], writes=[wdb])
        YTs = YT.ap.rearrange("p (g s) -> p g s", g=4)

        def ldx(tl):
            xt = xts[tl % 3]
            dma("sp", xt.ap, x_src[tl * 128:(tl + 1) * 128, :], reads=[X_src], writes=[xt])

        ldx(0)
        ldx(1)
        for tl in range(NT):
            xt = xts[tl % 3]
            b0 = (tl % 3) * 2
            pm = T(psum[:, b0 * 512:(b0 + 2) * 512])
            for hf in range(2):
                bank = PB[b0 + hf]
                for c in range(8):
                    src_t, src_ap = (YT, YTs[:, c, tl * 128:(tl + 1) * 128]) if c < 4 else (ygT, ygv[:, c - 4, tl * 128:(tl + 1) * 128])
                    P.op("pe", (lambda o, a, b_, st, sp: lambda e: e.matmul(o, lhsT=a, rhs=b_, start=st, stop=sp))(bank.ap, src_ap, woutv[:, c, hf * 512:(hf + 1) * 512], c == 0, c == 7), reads=[src_t, woutb], writes=[bank, pm])
            residual_out(pm, pm.ap, xt, t1s[tl % 2], xa_d[tl * 128:(tl + 1) * 128, :], D_xa, q="sp")
            for hf in range(2):
                PB[b0 + hf].b.rs.update(pm.b.rs)
            if tl + 2 < NT:
                ldx(tl + 2)
        P.barrier(skip_pool_dma=True)
        if stop_here(l, "D"):
            done = True
            break

        al = Alloc(arena, E_TR, E_WD)
        sbase = al.p
        uT = al.bf16(NF * 512)
        uTv = uT.ap.rearrange("p (f t) -> p f t", f=NF)
        hT = al.bf16(8 * 512)
        hTv = hT.ap.rearrange("p (c t) -> p c t", c=8)
        xns = [al.f32(1024), al.f32(1024)]
        xrs = [al.f32(1024), al.f32(1024)]
        hbs = [al.bf16(1024) for _ in range(4)]
        sgs = [al.bf16(512), al.bf16(512)]
        t1s = [al.f32(1024)]
        dma("sp", wpost.ap, norm_ffn_post[l:l + 1, :].to_broadcast([128, 1024]), writes=[wpost])
        def alias_stage(t_):
            n_ = T(t_.ap[:, 0:768])
            n_.b = t_.b
            return n_

        estg = [T(arena[:, E_WD + 11264:E_WD + 11264 + 768]), T(arena[:, E_WD + 11264 + 768:E_WD + 11264 + 1536]),
                alias_stage(xrs[0]), alias_stage(xrs[1]), alias_stage(t1s[0])]
        assert E_WD + 11264 + 1536 <= ARENA_COLS
        ekk = [0]

        def ld_fblk(bi):
            fa, fb = FBLK[bi]
            w_ = (fb - fa) * 128
            for (WT, WV, src) in ((WG, WGv, w_gate[l]), (WU, WUv, w_up[l])):
                for c in range(8):
                    st = estg[ekk[0] % len(estg)]
                    dma("sp", st.ap[:, 0:w_], src[c * 128:(c + 1) * 128, fa * 128:fb * 128], writes=[st])
                    cp("act" if ekk[0] % 2 else "dve", WT[bi], WV[bi][:, c, :], st, st.ap[:, 0:w_])
                    ekk[0] += 1

        gain_e = nwv[:, l, 1, :]

        def prep_norm_e(g, j):
            tl = g * 4 + j
            xt = xns[j % 2]
            dma("sp", xt.ap, xa_d[tl * 128:(tl + 1) * 128, :], reads=[D_xa], writes=[xt])
            norm_hb(xt, hbs[j])

        def prep_tr_e():
            for j in range(4):
                transpose_to_hT(hbs[j], PB[6 + j % 2], hT, hTv[:, :, j * 128:(j + 1) * 128], "act", gain_ap=gain_e)

        for j in range(4):
            prep_norm_e(0, j)
        prep_tr_e()
        ntile = 0
        for g in range(8):
            for f in range(NF):
                pg = PB[(f % 2) * 2]
                pu = PB[1 + (f % 2) * 2]
                for c in range(8):
                    mm(pg, pg.ap, WG[fblk_of[f]], WGv[fblk_of[f]][:, c, (f - FBLK[fblk_of[f]][0]) * 128:(f - FBLK[fblk_of[f]][0] + 1) * 128], hT, hTv[:, c, :], c == 0, c == 7)
                for c in range(8):
                    mm(pu, pu.ap, WU[fblk_of[f]], WUv[fblk_of[f]][:, c, (f - FBLK[fblk_of[f]][0]) * 128:(f - FBLK[fblk_of[f]][0] + 1) * 128], hT, hTv[:, c, :], c == 0, c == 7)
                sg = sgs[f % 2]
                act(sg, sg.ap, pg, pg.ap, AF.Silu)
                tt("dve", uT, uTv[:, f, :], pu, pu.ap, sg, sg.ap, ALU.mult)
                if g == 0 and f in (0, 4, 10):
                    ld_fblk({0: 1, 4: 2, 10: 3}[f])
                if g + 1 < 8 and f in (3, 8, 13, 18):
                    prep_norm_e(g + 1, (f - 3) // 5)
            if g + 1 < 8:
                prep_tr_e()
            for j in range(4):
                tl = g * 4 + j
                xr = xrs[j % 2]
                dma("sp", xr.ap, xa_d[tl * 128:(tl + 1) * 128, :], reads=[D_xa], writes=[xr])
                b0 = 4 + (ntile % 2) * 2
                ntile += 1
                pm = T(psum[:, b0 * 512:(b0 + 2) * 512])
                for hf in range(2):
                    bank = PB[b0 + hf]
                    for f in range(NF):
                        P.op("pe", (lambda o, a, b, st, sp: lambda e: e.matmul(o, lhsT=a, rhs=b, start=st, stop=sp))(bank.ap, uTv[:, f, j * 128:(j + 1) * 128], wdv[:, f, hf * 512:(hf + 1) * 512], f == 0, f == NF - 1), reads=[uT, wdb], writes=[bank, pm])
                residual_out(pm, pm.ap, xr, t1s[0], x_fin[tl * 128:(tl + 1) * 128, :], X_fin, q="sp")
                for hf in range(2):
                    PB[b0 + hf].b.rs.update(pm.b.rs)
        P.barrier()
    P.emit()
    return nc


def _consts():
    bf = ml_dtypes.bfloat16
    c = {}
    c["c_ident"] = np.eye(128, dtype=np.float32).astype(bf)
    c["c_ones"] = np.ones((128, 128), dtype=np.float32).astype(bf)
    c["c_onesf"] = np.ones((1, 128), dtype=np.float32)
    i = np.arange(128)
    ang = 2 * np.pi * np.outer(i, i) / 128.0
    c["c_dftc"] = np.concatenate([np.cos(ang), np.sin(ang)], axis=1).astype(np.float32).astype(bf)
    N = S
    s = np.arange(N // 8, dtype=np.float64)[:, None]
    kp = np.arange(N // 8, dtype=np.float64)[None, :]
    nrm = 1.0 / np.sqrt(N * 128.0)
    m = np.zeros((8, 2, N // 8, N // 8), dtype=np.float32)
    sign = [-1.0, -1.0, 1.0, 1.0]
    for rho in range(8):
        ph = 2 * np.pi * np.mod(s * (8 * kp + rho), N) / N
        m[rho, 0] = np.cos(ph) * nrm
        m[rho, 1] = np.sin(ph) * nrm * sign[rho % 4]
    c["c_dfts"] = m.astype(bf)
    j = np.arange(128)[:, None]
    t = np.arange(128)[None, :]
    lf = (j <= t).astype(np.float32)
    ub = (j >= t).astype(np.float32)
    c["c_tri"] = np.stack([lf, ub]).astype(bf)
    c["c_mask"] = np.stack([np.tile(lf, (1, 4)), np.tile(ub, (1, 4))]).astype(np.float32)
    return c


_NC = [None]


def kernel(**inputs):
    if _NC[0] is None:
        _NC[0] = build_program()
    nc = _NC[0]
    consts = _consts()
    x = np.ascontiguousarray(inputs["x"], dtype=np.float32)
    shared = {k: np.ascontiguousarray(v, dtype=np.float32) for k, v in inputs.items() if k != "x"}
    in_maps = []
    for i in range(8):
        m = dict(shared)
        m.update(consts)
        m["x"] = x[i]
        in_maps.append(m)
    res = run_bass_kernel_spmd(nc, in_maps, core_ids=list(range(8)))
    return np.stack([r["y"] for r in res.results], axis=0).astype(np.float32)
```
